# Optimizing a Trainium2 kernel written in Bass

```python
import jax
import jax.numpy as jnp
from jax import lax
import numpy as np

D_MODEL = 1024
BATCH = 16
SEQ = 256
DEPTH = 2
DEC_BATCH = 8
DEC_SEQ = 4096
PAST_LEN = 512

GRID_W = 64
BRANCH_W = D_MODEL // 2
N_HEADS = 8
HEAD_DIM = BRANCH_W // N_HEADS
ATTN_W = N_HEADS * HEAD_DIM
POOL_W = BRANCH_W
POOL_WINDOWS = (2, 4, 8, 16)
POOL_GROUP = POOL_W // len(POOL_WINDOWS)
LRU_W = BRANCH_W
LRU_BLOCKS = 8
LRU_BLOCK = LRU_W // LRU_BLOCKS
LRU_C = 8.0
CONV_W = 4
N_BRANCH = 3
D_FF = 4 * D_MODEL
WIN_ROWS_MAX = 8
WIN_COLS = 16
Q_COL_BLOCK = WIN_COLS
K_COL_BAND = 2 * WIN_COLS
ROPE_THETA = 10000.0
CTX_Q_BLOCK = 128
EPS = 1e-6
NEG_INF = -1e30
IN_COLS = POOL_W + 3 * ATTN_W + 2 * LRU_W + N_BRANCH * D_MODEL

kernel_name = 'hybrid_pool_natten_rglru_diffusion_step'


def rms_norm(x, g):
    xf = x.astype(jnp.float32)
    y = xf * lax.rsqrt(jnp.mean(xf * xf, axis=-1, keepdims=True) + EPS)
    return (y * g.astype(jnp.float32)).astype(x.dtype)


def ada_mod(cond, w_mod, b_mod):
    m = jax.nn.silu(cond) @ w_mod + b_mod
    return jnp.split(m[:, None, :], 6, axis=-1)


def pool_mixer(u, w_pool, pool_scale):
    B, L, _ = u.shape
    uf = u.astype(jnp.float32)
    csum = jnp.concatenate([jnp.zeros((B, 1, POOL_W), jnp.float32), jnp.cumsum(uf, axis=1)], axis=1)
    t = np.arange(L)
    outs = []
    for gi, w in enumerate(POOL_WINDOWS):
        lo = np.clip(t - w // 2, 0, L)
        hi = np.clip(t - w // 2 + w, 0, L)
        sl = slice(gi * POOL_GROUP, (gi + 1) * POOL_GROUP)
        mean = (csum[:, hi, sl] - csum[:, lo, sl]) / (hi - lo).astype(np.float32)[None, :, None]
        outs.append(mean - uf[:, :, sl])
    d = jnp.stack(outs, axis=2).astype(u.dtype)
    y = jnp.einsum('blgc,gcd->blgd', d, w_pool).reshape(B, L, POOL_W)
    return y * pool_scale


def conv_centred(u, w, b):
    L = u.shape[1]
    left = (CONV_W - 1) // 2
    up = jnp.pad(u, ((0, 0), (left, CONV_W - 1 - left), (0, 0)))
    return sum(up[:, j:j + L] * w[j] for j in range(CONV_W)) + b


def block_diag(x, w):
    B, L, _ = x.shape
    xb = x.reshape(B, L, LRU_BLOCKS, LRU_BLOCK)
    return jnp.einsum('blnc,ncd->blnd', xb, w).reshape(B, L, LRU_W)


def linear_scan(a, b, h0):
    b = b.at[:, 0].add(a[:, 0] * h0)

    def comb(e1, e2):
        a1, b1 = e1
        a2, b2 = e2
        return a1 * a2, a2 * b1 + b2

    _, h = lax.associative_scan(comb, (a, b), axis=1)
    return h


def rg_lru(xc, w_a, b_a, w_i, b_i, lam, h0):
    xf = xc.astype(jnp.float32)
    h0 = h0.astype(jnp.float32)
    hs = []
    finals = []
    for d in range(2):
        r = jax.nn.sigmoid(block_diag(xc, w_a[d]).astype(jnp.float32) + b_a[d].astype(jnp.float32))
        i = jax.nn.sigmoid(block_diag(xc, w_i[d]).astype(jnp.float32) + b_i[d].astype(jnp.float32))
        log_a = -LRU_C * r * jax.nn.softplus(-lam[d].astype(jnp.float32))
        a = jnp.exp(log_a)
        b = jnp.sqrt(-jnp.expm1(2.0 * log_a)) * (i * xf)
        if d == 0:
            h = linear_scan(a, b, h0[:, 0])
            finals.append(h[:, -1])
        else:
            h = jnp.flip(linear_scan(jnp.flip(a, 1), jnp.flip(b, 1), h0[:, 1]), 1)
            finals.append(h[:, 0])
        hs.append(h)
    return hs[0] + hs[1], jnp.stack(finals, axis=1)


def axial_rope(x):
    n = x.shape[1]
    t = np.arange(n)
    half = HEAD_DIM // 2
    nf = half // 2
    inv = ROPE_THETA ** (-np.arange(nf) / nf)

    def rot(xp, pos):
        ang = pos[:, None] * inv[None, :]
        cos = np.cos(ang).astype(np.float32)[None, :, None, :]
        sin = np.sin(ang).astype(np.float32)[None, :, None, :]
        x1, x2 = xp[..., :nf], xp[..., nf:]
        return jnp.concatenate([x1 * cos - x2 * sin, x2 * cos + x1 * sin], axis=-1)

    xf = x.astype(jnp.float32)
    out = jnp.concatenate([rot(xf[..., :half], t // GRID_W), rot(xf[..., half:], t % GRID_W)], axis=-1)
    return out.astype(x.dtype)


def context_attention(q, k, v):
    B, Lq = q.shape[:2]
    nb = Lq // CTX_Q_BLOCK
    qb = q.reshape(B, nb, CTX_Q_BLOCK, N_HEADS, HEAD_DIM).transpose(1, 0, 2, 3, 4)
    scale = HEAD_DIM ** -0.5

    def blk(qi):
        s = jnp.einsum('bqhd,bkhd->bhqk', qi, k).astype(jnp.float32) * scale
        p = jax.nn.softmax(s, axis=-1).astype(v.dtype)
        return jnp.einsum('bhqk,bkhd->bqhd', p, v)

    o = lax.map(blk, qb)
    return o.transpose(1, 0, 2, 3, 4).reshape(B, Lq, ATTN_W)


def neighbourhood_attention(q, k, v, kc, vc, rel_bias):
    B, N = q.shape[:2]
    rows = N // GRID_W
    wr = min(WIN_ROWS_MAX, rows)
    ncb = GRID_W // Q_COL_BLOCK
    qg = q.reshape(B, rows, ncb, Q_COL_BLOCK, N_HEADS, HEAD_DIM)
    kg = k.reshape(B, rows, GRID_W, N_HEADS, HEAD_DIM)
    vg = v.reshape(B, rows, GRID_W, N_HEADS, HEAD_DIM)
    band0 = np.clip(np.arange(ncb) * Q_COL_BLOCK - WIN_COLS // 2, 0, GRID_W - K_COL_BAND)
    kcol = band0[:, None] + np.arange(K_COL_BAND)
    qcol = np.arange(ncb)[:, None] * Q_COL_BLOCK + np.arange(Q_COL_BLOCK)
    c0 = np.clip(qcol - WIN_COLS // 2, 0, GRID_W - WIN_COLS)
    col_mask = (kcol[:, None, :] >= c0[:, :, None]) & (kcol[:, None, :] < c0[:, :, None] + WIN_COLS)
    dc_idx = np.clip(kcol[:, None, :] - qcol[:, :, None] + WIN_COLS - 1, 0, 2 * WIN_COLS - 2)
    bias_cols = rel_bias[:, :, dc_idx].astype(jnp.float32)
    scale = HEAD_DIM ** -0.5
    n_loc = wr * K_COL_BAND

    def row(r):
        r0 = jnp.clip(r - wr // 2, 0, rows - wr)
        qr = lax.dynamic_index_in_dim(qg, r, axis=1, keepdims=False)
        kr = lax.dynamic_slice_in_dim(kg, r0, wr, axis=1)[:, :, kcol]
        vr = lax.dynamic_slice_in_dim(vg, r0, wr, axis=1)[:, :, kcol]
        s_loc = jnp.einsum('bjqhd,brjkhd->bhjqrk', qr, kr).astype(jnp.float32) * scale
        dr_idx = r0 + jnp.arange(wr) - r + WIN_ROWS_MAX - 1
        bias = jnp.take(bias_cols, dr_idx, axis=1).transpose(0, 2, 3, 1, 4)
        s_loc = jnp.where(col_mask[:, :, None, :], s_loc + bias, NEG_INF)
        s_loc = s_loc.reshape(B, N_HEADS, ncb, Q_COL_BLOCK, n_loc)
        s_ctx = jnp.einsum('bjqhd,bkhd->bhjqk', qr, kc).astype(jnp.float32) * scale
        p = jax.nn.softmax(jnp.concatenate([s_loc, s_ctx], axis=-1), axis=-1).astype(v.dtype)
        p_loc = p[..., :n_loc].reshape(B, N_HEADS, ncb, Q_COL_BLOCK, wr, K_COL_BAND)
        o = (jnp.einsum('bhjqrk,brjkhd->bjqhd', p_loc, vr)
             + jnp.einsum('bhjqk,bkhd->bjqhd', p[..., n_loc:], vc))
        return o.reshape(B, GRID_W, ATTN_W)

    o = lax.map(row, jnp.arange(rows))
    return o.transpose(1, 0, 2, 3).reshape(B, N, ATTN_W)


def trunk_layer(x, cond, ctx_k, ctx_v, ctx_h, w_mod, b_mod, norm1_g, w_in, q_norm_g, k_norm_g,
                w_pool, pool_scale, rel_bias, conv_w, conv_b, lru_w_a, lru_b_a, lru_w_i, lru_b_i,
                lru_lambda, w_branch, w_out, norm2_g, w_up, w_down):
    B, L, _ = x.shape
    shift1, scale1, gate1, shift2, scale2, gate2 = ada_mod(cond, w_mod, b_mod)
    h = rms_norm(x, norm1_g) * (1 + scale1) + shift1
    z = h @ w_in
    splits = [POOL_W, POOL_W + ATTN_W, POOL_W + 2 * ATTN_W, POOL_W + 3 * ATTN_W,
              POOL_W + 3 * ATTN_W + LRU_W, POOL_W + 3 * ATTN_W + 2 * LRU_W]
    u_pool, q, k, v, u_lru, u_gate, gates = jnp.split(z, splits, axis=-1)
    heads = (B, L, N_HEADS, HEAD_DIM)
    q = rms_norm(q.reshape(heads), q_norm_g)
    k = rms_norm(k.reshape(heads), k_norm_g)
    v = v.reshape(heads)
    pool_o = pool_mixer(u_pool, w_pool, pool_scale)
    xc = conv_centred(u_lru, conv_w, conv_b)
    if ctx_k is None:
        attn_o = context_attention(q, k, v)
        h0 = jnp.zeros((B, 2, LRU_W), jnp.float32)
    else:
        attn_o = neighbourhood_attention(axial_rope(q), axial_rope(k), v, ctx_k, ctx_v, rel_bias)
        h0 = ctx_h
    h_seq, h_final = rg_lru(xc, lru_w_a, lru_b_a, lru_w_i, lru_b_i, lru_lambda, h0)
    lru_o = h_seq.astype(x.dtype) * jax.nn.gelu(u_gate)
    g = jax.nn.sigmoid(gates.reshape(B, L, N_BRANCH, D_MODEL).astype(jnp.float32)).astype(x.dtype)
    merged = (g[:, :, 0] * (pool_o @ w_branch[0]) + g[:, :, 1] * (attn_o @ w_branch[1])
              + g[:, :, 2] * (lru_o @ w_branch[2]))
    x = x + gate1 * (merged @ w_out)
    h2 = rms_norm(x, norm2_g) * (1 + scale2) + shift2
    x = x + gate2 * (jnp.square(jax.nn.relu(h2 @ w_up)) @ w_down)
    return x, k, v, h_final.astype(x.dtype)


def setup_inputs(seed: int = 0) -> dict:
    key = jax.random.key(seed)
    ks = jax.random.split(key, 32)
    f32 = jnp.float32

    def nrm(k, shape, scale):
        return jax.random.normal(k, shape, f32) * scale

    a0 = jax.random.uniform(ks[22], (DEPTH, 2, LRU_W), f32, 0.9, 0.999)
    s = a0 ** (1.0 / LRU_C)
    return {
        'x_prompt': nrm(ks[0], (BATCH, SEQ, D_MODEL), 1.0),
        'x_sample': nrm(ks[1], (DEC_BATCH, DEC_SEQ, D_MODEL), 1.0),
        'cache_k': nrm(ks[2], (DEC_BATCH, DEPTH, PAST_LEN, N_HEADS, HEAD_DIM), 1.0),
        'cache_v': nrm(ks[3], (DEC_BATCH, DEPTH, PAST_LEN, N_HEADS, HEAD_DIM), 1.0),
        'state_h': nrm(ks[4], (DEC_BATCH, DEPTH, 2, LRU_W), 0.5),
        'c': nrm(ks[5], (DEC_BATCH, D_MODEL), 1.0),
        'c_ctx': nrm(ks[6], (D_MODEL,), 1.0),
        'w_mod': nrm(ks[7], (DEPTH, D_MODEL, 6 * D_MODEL), 0.5 * D_MODEL ** -0.5),
        'b_mod': nrm(ks[8], (DEPTH, 6 * D_MODEL), 0.01),
        'norm1_g': 1.0 + nrm(ks[9], (DEPTH, D_MODEL), 0.01),
        'w_in': nrm(ks[10], (DEPTH, D_MODEL, IN_COLS), D_MODEL ** -0.5),
        'q_norm_g': 1.0 + nrm(ks[11], (DEPTH, HEAD_DIM), 0.01),
        'k_norm_g': 1.0 + nrm(ks[12], (DEPTH, HEAD_DIM), 0.01),
        'w_pool': nrm(ks[13], (DEPTH, len(POOL_WINDOWS), POOL_GROUP, POOL_GROUP), POOL_GROUP ** -0.5),
        'pool_scale': 1.0 + nrm(ks[14], (DEPTH, POOL_W), 0.01),
        'rel_bias': nrm(ks[15], (DEPTH, N_HEADS, 2 * WIN_ROWS_MAX - 1, 2 * WIN_COLS - 1), 0.02),
        'conv_w': nrm(ks[16], (DEPTH, CONV_W, LRU_W), CONV_W ** -0.5),
        'conv_b': nrm(ks[17], (DEPTH, LRU_W), 0.01),
        'lru_w_a': nrm(ks[18], (DEPTH, 2, LRU_BLOCKS, LRU_BLOCK, LRU_BLOCK), LRU_BLOCK ** -0.5),
        'lru_b_a': nrm(ks[19], (DEPTH, 2, LRU_W), 0.01),
        'lru_w_i': nrm(ks[20], (DEPTH, 2, LRU_BLOCKS, LRU_BLOCK, LRU_BLOCK), LRU_BLOCK ** -0.5),
        'lru_b_i': nrm(ks[21], (DEPTH, 2, LRU_W), 0.01),
        'lru_lambda': jnp.log(s) - jnp.log1p(-s),
        'w_branch': nrm(ks[23], (DEPTH, N_BRANCH, BRANCH_W, D_MODEL), BRANCH_W ** -0.5),
        'w_out': nrm(ks[24], (DEPTH, D_MODEL, D_MODEL), D_MODEL ** -0.5),
        'norm2_g': 1.0 + nrm(ks[25], (DEPTH, D_MODEL), 0.01),
        'w_up': nrm(ks[26], (DEPTH, D_MODEL, D_FF), D_MODEL ** -0.5),
        'w_down': nrm(ks[27], (DEPTH, D_FF, D_MODEL), D_FF ** -0.5),
    }


def reference(x_prompt, x_sample, cache_k, cache_v, state_h, c, c_ctx, w_mod, b_mod, norm1_g, w_in,
              q_norm_g, k_norm_g, w_pool, pool_scale, rel_bias, conv_w, conv_b, lru_w_a, lru_b_a,
              lru_w_i, lru_b_i, lru_lambda, w_branch, w_out, norm2_g, w_up, w_down):
    stacked = (w_mod, b_mod, norm1_g, w_in, q_norm_g, k_norm_g, w_pool, pool_scale, rel_bias,
               conv_w, conv_b, lru_w_a, lru_b_a, lru_w_i, lru_b_i, lru_lambda, w_branch, w_out,
               norm2_g, w_up, w_down)
    y_prompt = x_prompt
    ks, vs, hs = [], [], []
    for l in range(DEPTH):
        lp = [p[l] for p in stacked]
        y_prompt, k_l, v_l, h_l = trunk_layer(y_prompt, c_ctx[None, :], None, None, None, *lp)
        ks.append(k_l)
        vs.append(v_l)
        hs.append(h_l)
    y_sample = x_sample
    for l in range(DEPTH):
        lp = [p[l] for p in stacked]
        y_sample, _, _, _ = trunk_layer(y_sample, c, cache_k[:, l], cache_v[:, l], state_h[:, l], *lp)
    new_cache_k = jnp.stack(ks, axis=1)
    new_cache_v = jnp.stack(vs, axis=1)
    new_state_h = jnp.stack(hs, axis=1)
    return (y_prompt, y_sample, new_cache_k, new_cache_v, new_state_h)
```

```python
import numpy as np
from contextlib import ExitStack
import concourse.bass as bass
import concourse.mybir as mybir
from concourse.bass_utils import run_bass_kernel_spmd

F32 = mybir.dt.float32
BF16 = mybir.dt.bfloat16
AF = mybir.ActivationFunctionType
ALU = mybir.AluOpType

EPOCH = 8192
NSLOT = 24


class T:
    __slots__ = ("lw", "rd", "name")

    def __init__(self, name=""):
        self.lw = None
        self.rd = {}
        self.name = name


class Prog:
    ENGS = ("pe", "act", "dve", "pool", "sp")

    def __init__(self, nc):
        self.nc = nc
        self.stream = {e: [] for e in self.ENGS}
        self.count = {e: 0 for e in self.ENGS}
        self.known = {e: {} for e in self.ENGS}
        self.slot_val = [0] * (3 * NSLOT)
        self.slot_next = {"sp": 0, "pool": 0, "act": 0}

    def _eng_event(self, eng, idx):
        return ((eng, (idx - 1) // EPOCH), (idx - 1) % EPOCH + 1)

    def _need(self, eng, ev, waits):
        if ev is None:
            return
        key, val = ev
        if self.known[eng].get(key, 0) >= val:
            return
        if waits.get(key, 0) < val:
            waits[key] = val

    def _collect(self, eng, reads, writes):
        waits = {}
        for t in reads:
            self._need(eng, t.lw, waits)
        for t in writes:
            self._need(eng, t.lw, waits)
            for k, v in t.rd.items():
                self._need(eng, (k, v), waits)
        return waits

    def _emit_waits(self, eng, waits):
        for key, val in waits.items():
            if key[0] == eng and eng == "pe":
                continue
            self.stream[eng].append(("wait", key, val))
            self.known[eng][key] = val
            if key[0] in self.ENGS:
                for ep in range(key[1]):
                    self.known[eng][(key[0], ep)] = EPOCH

    def _record(self, ev, reads, writes):
        k, v = ev
        for t in reads:
            if t.rd.get(k, 0) < v:
                t.rd[k] = v
        for t in writes:
            t.lw = ev
            t.rd = {}

    @staticmethod
    def _flat(ts):
        out = []
        for t in ts:
            if isinstance(t, (list, tuple)):
                out.extend(Prog._flat(t))
            else:
                out.append(t)
        return out

    def op(self, eng, fn, reads=(), writes=(), inc=True):
        reads, writes = self._flat(reads), self._flat(writes)
        waits = self._collect(eng, reads, writes)
        self._emit_waits(eng, waits)
        if inc:
            self.count[eng] += 1
            ev = self._eng_event(eng, self.count[eng])
            self.stream[eng].append(("op", fn, ev[0]))
        else:
            ev = self._eng_event(eng, self.count[eng] + 1)
            self.stream[eng].append(("op", fn, None))
        self._record(ev, reads, writes)
        return ev

    def dma(self, q, out_ap, in_ap, reads=(), writes=()):
        reads, writes = self._flat(reads), self._flat(writes)
        base = {"sp": 0, "pool": NSLOT, "act": 2 * NSLOT}[q]
        nsl = NSLOT if q != "act" else 12
        s = base + self.slot_next[q]
        self.slot_next[q] = (self.slot_next[q] + 1) % nsl
        key = ("dma", s)
        waits = self._collect(q, reads, writes)
        if self.slot_val[s] > 0:
            self._need(q, (key, self.slot_val[s]), waits)
        self._emit_waits(q, waits)
        self.slot_val[s] += 16
        ev = (key, self.slot_val[s])
        self.stream[q].append(("dma", out_ap, in_ap, key))
        self._record(ev, reads, writes)
        return ev

    def barrier(self):
        evs = []
        for e in self.ENGS:
            if self.count[e] > 0:
                evs.append(self._eng_event(e, self.count[e]))
        for s, v in enumerate(self.slot_val):
            if v > 0:
                evs.append((("dma", s), v))
        for e in self.ENGS:
            waits = {}
            for ev in evs:
                if ev[0][0] == e:
                    continue
                self._need(e, ev, waits)
            self._emit_waits(e, waits)

    def emit(self, sem_alloc):
        nc = self.nc
        keys = set()
        for e in self.ENGS:
            for it in self.stream[e]:
                if it[0] == "wait":
                    keys.add(it[1])
                elif it[0] == "op" and it[2] is not None:
                    keys.add(it[2])
                elif it[0] == "dma":
                    keys.add(it[3])
        keys = sorted(keys, key=str)
        sems = {k: sem_alloc("s_%s_%s" % (k[0], k[1])) for k in keys}
        engobj = {"pe": "tensor", "act": "scalar", "dve": "vector", "pool": "gpsimd", "sp": "sync"}
        with nc.Block() as block:
            for e in self.ENGS:
                items = self.stream[e]
                if not items:
                    continue

                def body(eobj, items=items):
                    for it in items:
                        if it[0] == "wait":
                            eobj.wait_ge(sems[it[1]], it[2])
                        elif it[0] == "op":
                            ins = it[1](eobj)
                            if it[2] is not None:
                                ins.then_inc(sems[it[2]], 1)
                        else:
                            eobj.dma_start(out=it[1], in_=it[2]).then_inc(sems[it[3]], 16)

                getattr(block, engobj[e])(body)


NT, NS, TT, NTILE = 4608, 4096, 512, 9
D, KC = 1024, 8
NL = 2
EPS = 1e-6
PADN = 4672
P_S0, P_P0, P_PST = 16, 4128, 272
NEG = -240000.0
E_MAX = 10
NY = 22
RB_PAD = 256
RB_LEN = RB_PAD + 8 * 15 * 31 + 1024

PV = {}
_o = 0
for _l in range(NL):
    for _n, _w in (("bmod", 48), ("g1", 8), ("g2", 8), ("qg", 1), ("qgp", 1), ("kg", 1), ("kgp", 1), ("pscale", 4),
                   ("convw", 16), ("convb", 4), ("ba", 8), ("bi", 8), ("lam", 8), ("h0", 8)):
        PV[(_n, _l)] = (_o, _w)
        _o += _w
PV["cond"] = (_o, 16)
_o += 16
NPV = _o

P_BND = [P_S0, P_S0 + NS - 8, P_P0, P_P0 + 248, P_P0 + P_PST, P_P0 + P_PST + 248]
TOP_KR = [0, 2, 4, 6, 8, 10]
BOT_KR = [52, 54, 56, 58, 60, 62]


def _r0(rq):
    return int(np.clip(rq - 4, 0, 56))


def host_consts():
    c = {}
    c["ident"] = np.eye(128, dtype=np.float32)
    bo = np.zeros((128, 128), np.float32)
    bo[:64, :64] = 1.0
    bo[64:, 64:] = 1.0
    c["bones"] = bo
    pm = np.arange(128)
    pm = (pm // 16 ^ 1) * 16 + pm % 16
    pf = np.zeros((128, 128), np.float32)
    pf[pm, np.arange(128)] = 1.0
    c["permf"] = pf
    t = np.arange(NS)
    nf = 16
    inv = 10000.0 ** (-np.arange(nf) / nf)
    C = np.zeros((128, NS), np.float32)
    S = np.zeros((128, NS), np.float32)
    for p in range(128):
        d = p % 64
        pos = (t // 64) if d < 32 else (t % 64)
        dd = d % 32
        i = dd % 16
        ang = pos * inv[i]
        C[p] = np.cos(ang).astype(np.float32)
        sn = np.sin(ang).astype(np.float32)
        S[p] = -sn if dd < 16 else sn
    c["ropeC"], c["ropeS"] = C, S
    kcol = np.arange(64)[:, None]
    qcol = np.arange(64)[None, :]
    c0 = np.clip(qcol - 8, 0, 48)
    colv = (kcol >= c0) & (kcol < c0 + 16)
    mint = np.full((128, NY, 64), NEG, np.float32)
    for half in range(2):
        for y in range(NY):
            e = E_MAX - y
            if -4 <= e + half <= 3:
                mint[half * 64:(half + 1) * 64, y, :] = np.where(colv, 0.0, NEG)
    c["mint"] = mint.reshape(128, NY * 64)
    medge = np.full((12, 128, 8, 64), NEG, np.float32)
    for blk, (rq0, krs) in enumerate(((0, TOP_KR), (56, BOT_KR))):
        for j, kr in enumerate(krs):
            for half in range(2):
                for rl in range(8):
                    rq = rq0 + rl
                    k = kr + half
                    if _r0(rq) <= k <= _r0(rq) + 7:
                        medge[blk * 6 + j, half * 64:(half + 1) * 64, rl, :] = np.where(colv, 0.0, NEG)
    c["medge"] = medge.reshape(12, 128, 512)
    pinv = np.zeros((4, PADN), np.float32)
    for gi, w in enumerate((2, 4, 8, 16)):
        for (start, L) in ((P_S0, NS), (P_P0, 256), (P_P0 + P_PST, 256)):
            tt = np.arange(L)
            lo = np.clip(tt - w // 2, 0, L)
            hi = np.clip(tt - w // 2 + w, 0, L)
            pinv[gi, start:start + L] = 1.0 / (hi - lo).astype(np.float32)
    c["pinv"] = pinv
    ivt = np.zeros((4, 6, 8), np.float32)
    for gi in range(4):
        for bi, pos in enumerate(P_BND):
            ivt[gi, bi] = pinv[gi, pos:pos + 8]
    c["ivt"] = np.ascontiguousarray(np.broadcast_to(ivt.reshape(1, 192), (128, 192)))
    return c


def V(ap, off, dims, p0=0, pn=None):
    (ps_, pcount) = ap.ap[0]
    if pn is None:
        pn = pcount - p0
    return bass.AP(ap.tensor, ap.offset + p0 * ps_ + off, [[ps_, pn]] + [[s, c] for (s, c) in dims])


class Arena:
    def __init__(self, tens, ncols):
        self.t = tens
        self.n = ncols
        self.top = 0

    def alloc(self, n, dt=F32):
        words = n if dt == F32 else (n + 1) // 2
        assert self.top + words <= self.n, "arena overflow %d + %d > %d" % (self.top, words, self.n)
        ap = self.t[:, self.top:self.top + words]
        self.top += words
        if dt != F32:
            ap = ap.bitcast(dt)[:, 0:n]
        return ap


def build(nl=NL, stop=None, debug=False):
    nc = bass.Bass("TRN2", target_bir_lowering=False)
    kin = "ExternalInput"
    kout = "ExternalOutput"
    dbg = kout if debug else "Internal"

    def din(name, shape, dt=F32):
        return nc.dram_tensor(name, list(shape), dt, kind=kin).ap()

    def dscr(name, shape, dt):
        if debug:
            return nc.dram_tensor(name, list(shape), dt, kind=kout).ap()
        return nc.dram_tensor(name, list(shape), dt).ap()

    x_d = din("x", [NT, D])
    ck_d = din("ck", [NL, 512, 512])
    cv_d = din("cv", [NL, 512, 512])
    pv_d = din("pvec", [128, NPV])
    wmod_d = din("w_mod", [NL, D, 6 * D])
    win_d = din("w_in", [NL, D, 6144])
    wbr_d = din("w_branch", [NL, 3, 512, D])
    wout_d = din("w_out", [NL, D, D])
    wup_d = din("w_up", [NL, D, 4 * D])
    wdn_d = din("w_down", [NL, 4 * D, D])
    wpool_d = din("w_pool", [NL, 4, 128, 128])
    lbd_d = din("lru_bd", [NL, 2, 2, 4, 128, 128])
    ident_d = din("ident", [128, 128])
    bones_d = din("bones", [128, 128])
    permf_d = din("permf", [128, 128])
    ropeC_d = din("ropeC", [128, NS])
    ropeS_d = din("ropeS", [128, NS])
    rbp_d = din("rbp", [NL, RB_LEN])
    mint_d = din("mint", [128, NY * 64])
    medge_d = din("medge", [12, 128, 512])
    pinv_d = din("pinv", [4, PADN])
    ivt_d = din("ivt", [128, 192])

    y_d = nc.dram_tensor("y", [NT, D], F32, kind=kout).ap()
    nk_d = nc.dram_tensor("nk", [2, NL, 256, 512], F32, kind=kout).ap()
    nv_d = nc.dram_tensor("nv", [2, NL, 256, 512], F32, kind=kout).ap()
    nh_d = nc.dram_tensor("nh", [32, 128], F32, kind=kout).ap()

    XT = dscr("XT", [D, NT], F32)
    ZP = dscr("ZP", [512, NT], F32)
    QR = dscr("QR", [512, NT], F32)
    KR = dscr("KR", [512, NT], F32)
    UL = dscr("UL", [512, NT], F32)
    UG = dscr("UG", [512, NT], BF16)
    GS = dscr("GS", [3072, NT], BF16)
    VS = dscr("VS", [NT, 520], BF16)
    QT = dscr("QT", [512, NT], BF16)
    KT = dscr("KT", [512, NT], BF16)
    PO = dscr("PO", [512, NT], BF16)
    AO = dscr("AO", [512, NT], BF16)
    LO = dscr("LO", [512, NT], BF16)
    H2 = dscr("H2", [D, NT], BF16)

    es = ExitStack()
    with es:
        PERS = 1536
        ARW = 51600
        pers_t = es.enter_context(nc.sbuf_tensor("pers", [128, PERS], F32))
        arena_t = es.enter_context(nc.sbuf_tensor("arena", [128, ARW], F32))
        psb = [es.enter_context(nc.psum_tensor("ps%d" % i, [128, 512], F32)) for i in range(8)]
        ps = [p[:, :] for p in psb]
        pst = [T("ps%d" % i) for i in range(8)]
        P = Prog(nc)
        PA = Arena(pers_t, PERS)
        A = Arena(arena_t, ARW)

        pv = PA.alloc(NPV)
        pv_t = T("pv")
        ident = PA.alloc(128)
        ident_t = T("ident")
        ones_b = PA.alloc(128, BF16)
        ident_b = PA.alloc(128, BF16)
        ones_f = PA.alloc(64)
        bones_b = PA.alloc(128, BF16)
        cst_t = T("cst")
        modv = [PA.alloc(96) for _ in range(NL)]
        A1 = [PA.alloc(16) for _ in range(NL)]
        A2 = [PA.alloc(16) for _ in range(NL)]
        cl8 = [PA.alloc(8) for _ in range(NL)]
        cltmp = [PA.alloc(16) for _ in range(NL)]
        fin = PA.alloc(32)
        fin_t = T("fin")
        mod_t = [T("mod%d" % l) for l in range(NL)]

        def pvc(name, l, i=0, n=1):
            o, w = PV[(name, l)]
            return pv[:, o + i:o + i + n]

        P.dma("sp", pv, pv_d, writes=[pv_t])
        P.dma("sp", ident, ident_d, writes=[ident_t])
        P.dma("pool", bones_b, bones_d, writes=[cst_t])
        P.op("dve", lambda e: e.memset(ones_b, 1.0), writes=[cst_t])
        P.op("dve", lambda e: e.tensor_copy(out=ident_b, in_=ident), reads=[ident_t], writes=[cst_t])
        P.op("dve", lambda e: e.memset(ones_f, 1.0), writes=[cst_t])
        P.op("dve", lambda e: e.memset(fin, 0.0), writes=[fin_t])
        epsc = PA.alloc(1)
        onec = PA.alloc(1)
        P.op("dve", lambda e: e.memset(epsc, EPS), writes=[cst_t])
        P.op("dve", lambda e: e.memset(onec, 1.0), writes=[cst_t])

        def modcol(l, j, grp, c):
            return modv[l][:, j * 48 + grp * 8 + c: j * 48 + grp * 8 + c + 1]

        sc = PA.alloc(16, BF16)
        sc_t = T("sc")
        MBANK = 6
        mstate = {"n": 0}

        def m_setup():
            co, _ = PV["cond"]
            P.op("act", lambda e: e.activation(out=sc, in_=pv[:, co:co + 16], func=AF.Silu), reads=[pv_t], writes=[sc_t])

        def m_slabs(l, s_list, slabs, slab_t):
            for s_ in s_list:
                b = mstate["n"] % 2
                mstate["n"] += 1
                P.dma("pool", V(slabs[b], 0, [(512, 8), (1, 512)]),
                      wmod_d[l, :, s_ * 512:(s_ + 1) * 512].rearrange("(kc p) n -> p kc n", p=128), writes=[slab_t[b]])
                for j in range(4):
                    col = s_ * 4 + j
                    for kc in range(8):
                        P.op("pe", lambda e, b=b, j=j, kc=kc, col=col, slabs=slabs: e.matmul(
                            ps[MBANK][:, 2 * col:2 * col + 2], lhsT=slabs[b][:, kc * 512 + j * 128: kc * 512 + j * 128 + 128],
                            rhs=sc[:, 2 * kc:2 * kc + 2], start=(kc == 0), stop=(kc == 7)),
                            reads=[slab_t[b], sc_t], writes=[pst[MBANK]], inc=(kc == 7))

        def m_finish(l):
            o, _ = PV[("bmod", l)]
            for j in range(2):
                P.op("dve", lambda e, l=l, j=j, o=o: e.tensor_tensor(
                    out=modv[l][:, j * 48:(j + 1) * 48], in0=V(ps[MBANK], j, [(2, 48)]), in1=pv[:, o:o + 48], op=ALU.add),
                    reads=[pst[MBANK], pv_t], writes=[mod_t[l]])
            for j in range(2):
                for (dst, grp, gname) in ((A1, 1, "g1"), (A2, 4, "g2")):
                    go, _ = PV[(gname, l)]
                    P.op("dve", lambda e, l=l, j=j, dst=dst, grp=grp, go=go: e.scalar_tensor_tensor(
                        out=dst[l][:, j * 8:(j + 1) * 8], in0=modv[l][:, j * 48 + grp * 8: j * 48 + grp * 8 + 8],
                        scalar=1.0, in1=pv[:, go:go + 8], op0=ALU.add, op1=ALU.mult),
                        reads=[mod_t[l], pv_t], writes=[mod_t[l]])
            lo_, _ = PV[("lam", l)]
            P.op("act", lambda e, l=l, lo_=lo_: e.activation(out=cltmp[l][:, 0:8], in_=pv[:, lo_:lo_ + 8], func=AF.Sigmoid),
                 reads=[pv_t], writes=[mod_t[l]])
            P.op("act", lambda e, l=l: e.activation(out=cltmp[l][:, 8:16], in_=cltmp[l][:, 0:8], func=AF.Ln),
                 reads=[mod_t[l]], writes=[mod_t[l]])
            P.op("dve", lambda e, l=l: e.tensor_scalar(out=cl8[l], in0=cltmp[l][:, 8:16], scalar1=8.0, scalar2=None, op0=ALU.mult),
                 reads=[mod_t[l]], writes=[mod_t[l]])

        def phase_M():
            A.top = 0
            m_setup()
            slabs = [A.alloc(8 * 512, BF16) for _ in range(2)]
            slab_t = [T("mslab0"), T("mslab1")]
            m_slabs(0, range(12), slabs, slab_t)
            m_finish(0)
            if nl > 1 and stop in ("M", "A", "A0"):
                m_slabs(1, range(12), slabs, slab_t)
                m_finish(1)

        def norm_tile(xT, xT_t, sq, sq_t, rs, rs_t, psn, Amod, Bcol, l, j, out_fn, out_reads_writes):
            raise NotImplementedError

        def phase_A(l):
            A.top = 0
            hT = A.alloc(8 * NT, BF16)
            hT_t = [T("hT%d" % i) for i in range(NTILE)]
            mark = A.top
            xin = [A.alloc(4 * D) for _ in range(2)] if l == 0 else None
            xin_t = [T("xin0"), T("xin1")]
            xT = [A.alloc(8 * TT) for _ in range(3)]
            xT_t = [T("xT0"), T("xT1"), T("xT2")]
            sq = [A.alloc(8 * TT, BF16) for _ in range(2)]
            sq_t = [T("sq0"), T("sq1")]
            rs = [A.alloc(TT) for _ in range(2)]
            rs_t = [T("rs0"), T("rs1")]
            def a0_stage1(i):
                b = i % 2
                j = 0 if i < 8 else 1
                xTv, xTv_t = xT[i % 3], xT_t[i % 3]
                if l == 0:
                    P.dma("sp", V(xin[b], 0, [(D, 4), (1, D)]),
                          x_d[i * TT:(i + 1) * TT, :].rearrange("(st p) f -> p st f", p=128), writes=[xin_t[b]])
                    for half in range(2):
                        for cc in range(4):
                            c = half * 4 + cc
                            for st in range(4):
                                P.op("pe", lambda e, b=b, c=c, cc=cc, st=st: e.transpose(
                                    ps[cc][:, st * 128:(st + 1) * 128], xin[b][:, st * D + c * 128: st * D + c * 128 + 128], ident),
                                    reads=[xin_t[b], ident_t], writes=[pst[cc]], inc=(st == 3))
                            eng = "dve"
                            if eng == "act":
                                P.op("act", lambda e, b=b, c=c, cc=cc: e.copy(out=xTv[:, c * TT:(c + 1) * TT], in_=ps[cc]),
                                     reads=[pst[cc]], writes=[xTv_t])
                            else:
                                P.op("dve", lambda e, b=b, c=c, cc=cc: e.tensor_copy(out=xTv[:, c * TT:(c + 1) * TT], in_=ps[cc]),
                                     reads=[pst[cc]], writes=[xTv_t])
                    P.dma("pool", XT[:, i * TT:(i + 1) * TT].rearrange("(c p) t -> p c t", p=128),
                          V(xTv, 0, [(TT, 8), (1, TT)]), reads=[xTv_t])
                else:
                    P.dma("sp", V(xTv, 0, [(TT, 8), (1, TT)]),
                          XT[:, i * TT:(i + 1) * TT].rearrange("(c p) t -> p c t", p=128), writes=[xTv_t])
                pn = 4 + b
                P.op("act", lambda e, b=b: e.activation(out=sq[b], in_=xTv, func=AF.Square), reads=[xTv_t], writes=[sq_t[b]])
                for c in range(8):
                    P.op("pe", lambda e, b=b, c=c, pn=pn: e.matmul(ps[pn], lhsT=ones_b, rhs=sq[b][:, c * TT:(c + 1) * TT],
                                                                  start=(c == 0), stop=(c == 7)),
                         reads=[sq_t[b], cst_t], writes=[pst[pn]], inc=(c == 7))

            def a0_stage2(i):
                b = i % 2
                j = 0 if i < 8 else 1
                pn = 4 + b
                xTv, xTv_t = xT[i % 3], xT_t[i % 3]
                P.op("act", lambda e, b=b, pn=pn: e.activation(out=rs[b], in_=ps[pn], func=AF.Ln, scale=1.0 / D, bias=epsc),
                     reads=[pst[pn], cst_t], writes=[rs_t[b]])
                P.op("act", lambda e, b=b: e.activation(out=rs[b], in_=rs[b], func=AF.Exp, scale=-0.5), reads=[rs_t[b]], writes=[rs_t[b]])
                P.op("dve", lambda e, b=b: e.tensor_tensor(out=V(xTv, 0, [(TT, 8), (1, TT)]), in0=V(xTv, 0, [(TT, 8), (1, TT)]),
                                                           in1=V(rs[b], 0, [(0, 8), (1, TT)]), op=ALU.mult),
                     reads=[rs_t[b], xTv_t], writes=[xTv_t])

            def a0_stage3(i):
                b = i % 2
                j = 0 if i < 8 else 1
                xTv, xTv_t = xT[i % 3], xT_t[i % 3]
                for c in range(8):
                    if c % 3 == 2:
                        P.op("dve", lambda e, b=b, c=c, i=i, j=j: e.tensor_scalar(
                            out=hT[:, c * NT + i * TT: c * NT + (i + 1) * TT], in0=xTv[:, c * TT:(c + 1) * TT],
                            scalar1=A1[l][:, j * 8 + c: j * 8 + c + 1], scalar2=modcol(l, j, 0, c), op0=ALU.mult, op1=ALU.add),
                            reads=[xTv_t, mod_t[l]], writes=[hT_t[i]])
                        continue
                    P.op("act", lambda e, b=b, c=c, i=i, j=j: e.activation(
                        out=hT[:, c * NT + i * TT: c * NT + (i + 1) * TT], in_=xTv[:, c * TT:(c + 1) * TT], func=AF.Identity,
                        scale=A1[l][:, j * 8 + c: j * 8 + c + 1], bias=modcol(l, j, 0, c)),
                        reads=[xTv_t, mod_t[l]], writes=[hT_t[i]])

            for i in range(NTILE + 2):
                if i < NTILE:
                    a0_stage1(i)
                if 0 <= i - 1 < NTILE:
                    a0_stage2(i - 1)
                if 0 <= i - 2 < NTILE:
                    a0_stage3(i - 2)
            if stop == "A0":
                return hT
            P.barrier()
            A.top = mark
            slabs = [A.alloc(8 * 512, BF16) for _ in range(2)]
            slab_t = [T("slab0"), T("slab1")]
            stg = [A.alloc(NT) for _ in range(2)]
            stg_t = [T("stg0"), T("stg1")]
            vst = [A.alloc(520, BF16) for _ in range(2)]
            vst_t = [T("vst0"), T("vst1")]
            for vb_ in range(2):
                P.op("pool", lambda e, vb_=vb_: e.memset(vst[vb_], 1.0), writes=[vst_t[vb_]])
            vsf = [A.alloc(512) for _ in range(2)]
            vsf_t = [T("vsf0"), T("vsf1")]
            gtmp = [(A.alloc(TT), A.alloc(TT)) for _ in range(2)]
            gtmp_t = [T("gt0"), T("gt1")]
            dests = {0: (ZP, F32, None), 1: (QR, F32, None), 2: (KR, F32, None), 4: (UL, F32, None), 5: (UG, BF16, "gelu")}
            nsub = 0
            npsum = 0
            order = [3, 1, 2, 0, 4, 5, 6, 7, 8, 9, 10, 11]
            import os as _os
            if _os.environ.get('KORDER'):
                order = [int(v) for v in _os.environ['KORDER'].split(',')]
            for si, s in enumerate(order):
                sb_ = si % 2
                P.dma("pool", V(slabs[sb_], 0, [(512, 8), (1, 512)]),
                      win_d[l, :, s * 512:(s + 1) * 512].rearrange("(kc p) n -> p kc n", p=128), writes=[slab_t[sb_]])
                if s == 3:
                    for tt in range(NT // 128):
                        pb = npsum % 8
                        npsum += 1
                        vb = tt % 2
                        for kc in range(8):
                            P.op("pe", lambda e, pb=pb, kc=kc, tt=tt, sb_=sb_: e.matmul(
                                ps[pb], lhsT=hT[:, kc * NT + tt * 128: kc * NT + tt * 128 + 128],
                                rhs=slabs[sb_][:, kc * 512:(kc + 1) * 512], start=(kc == 0), stop=(kc == 7)),
                                reads=[hT_t[tt // 4], slab_t[sb_]], writes=[pst[pb]], inc=(kc == 7))
                        P.op("act", lambda e, pb=pb, vb=vb: e.copy(out=V(vst[vb], 0, [(65, 8), (1, 64)]), in_=V(ps[pb], 0, [(64, 8), (1, 64)])),
                             reads=[pst[pb]], writes=[vst_t[vb]])
                        if _os.environ.get('KV') != '2':
                            P.dma("sp", VS[tt * 128:(tt + 1) * 128, :], vst[vb], reads=[vst_t[vb]])
                        if tt >= 32 and _os.environ.get('KV') != '1':
                            P.op("dve", lambda e, pb=pb, vb=vb: e.tensor_copy(out=vsf[vb], in_=ps[pb]), writes=[pst[pb], vsf_t[vb]])
                            sq_ = (tt - 32) // 2
                            hf = (tt - 32) % 2
                            P.dma("sp", nv_d[sq_, l, hf * 128:(hf + 1) * 128, :], vsf[vb], reads=[vsf_t[vb]])
                    continue
                for jj in range(4):
                    gb = nsub % 2
                    nsub += 1
                    if s in dests:
                        dst, dt, fn = dests[s]
                        drow = jj * 128
                    else:
                        dst, dt, fn = GS, BF16, "sig"
                        drow = (s - 6) * 512 + jj * 128
                    sview = stg[gb] if dt == F32 else stg[gb].bitcast(BF16)[:, 0:NT]
                    for i in range(NTILE):
                        pb = npsum % 8
                        npsum += 1
                        for kc in range(8):
                            P.op("pe", lambda e, pb=pb, kc=kc, i=i, sb_=sb_, jj=jj: e.matmul(
                                ps[pb], lhsT=slabs[sb_][:, kc * 512 + jj * 128: kc * 512 + jj * 128 + 128],
                                rhs=hT[:, kc * NT + i * TT: kc * NT + (i + 1) * TT], start=(kc == 0), stop=(kc == 7)),
                                reads=[hT_t[i], slab_t[sb_]], writes=[pst[pb]], inc=(kc == 7))
                        o_ap = sview[:, i * TT:(i + 1) * TT]
                        if fn == "gelu":
                            gelu_from_psum(P, ps[pb], pst[pb], o_ap, stg_t[gb], gtmp, gtmp_t, npsum)
                        elif fn == "sig":
                            P.op("act", lambda e, pb=pb, o_ap=o_ap: e.activation(out=o_ap, in_=ps[pb], func=AF.Sigmoid),
                                 reads=[pst[pb]], writes=[stg_t[gb]])
                        elif i % 2 == 0:
                            P.op("act", lambda e, pb=pb, o_ap=o_ap: e.copy(out=o_ap, in_=ps[pb]), reads=[pst[pb]], writes=[stg_t[gb]])
                        else:
                            P.op("dve", lambda e, pb=pb, o_ap=o_ap: e.tensor_copy(out=o_ap, in_=ps[pb]), reads=[pst[pb]], writes=[stg_t[gb]])
                    P.dma("sp", dst[drow:drow + 128, :], sview, reads=[stg_t[gb]])
            return hT

        gtmp = None
        gtmp_t = None

        def gelu_from_psum(P, psrc, psrc_t, o_ap, o_t, gtmp, gtmp_t, n):
            b = n % 2
            g0, g1 = gtmp[b]
            t = gtmp_t[b]
            P.op("act", lambda e: e.activation(out=g0, in_=psrc, func=AF.Square), reads=[psrc_t], writes=[t])
            P.op("dve", lambda e: e.tensor_scalar(out=g0, in0=g0, scalar1=0.044715, scalar2=1.0, op0=ALU.mult, op1=ALU.add),
                 reads=[t], writes=[t])
            P.op("dve", lambda e: e.tensor_tensor(out=g0, in0=g0, in1=psrc, op=ALU.mult), reads=[t, psrc_t], writes=[t])
            P.op("act", lambda e: e.activation(out=g1, in_=g0, func=AF.Sigmoid, scale=1.5957691216), reads=[t], writes=[t])
            P.op("dve", lambda e: e.tensor_tensor(out=o_ap, in0=g1, in1=psrc, op=ALU.mult), reads=[t, psrc_t], writes=[o_t])

        def ptile(buf, i, p0=0, pn=None):
            if i < 8:
                return V(buf, P_S0 + i * TT, [(1, TT)], p0=p0, pn=pn)
            return V(buf, P_P0, [(P_PST, 2), (1, 256)], p0=p0, pn=pn)

        def as3(ap512):
            return V(ap512, 0, [(256, 2), (1, 256)])

        def load_padded(q, buf, buf_t, src_rows, dt_note=None):
            P.dma(q, V(buf, P_S0, [(1, NS)]), src_rows[:, 0:NS], writes=[buf_t])
            P.dma(q, V(buf, P_P0, [(P_PST, 2), (1, 256)]), src_rows[:, NS:NT].rearrange("p (s t) -> p s t", s=2), writes=[buf_t])

        def phase_P(l):
            A.top = 0
            Ub = [A.alloc(PADN) for _ in range(2)]
            S1b = [A.alloc(PADN) for _ in range(2)]
            S2b = [A.alloc(PADN) for _ in range(2)]
            IVb = [None, None]
            Dbb = [A.alloc(PADN, BF16) for _ in range(2)]
            wpb = [A.alloc(128, BF16) for _ in range(2)]
            stg = [A.alloc(NT, BF16) for _ in range(2)]
            Ut = [T("U0"), T("U1")]
            S1t = [T("S10"), T("S11")]
            S2t = [T("S20"), T("S21")]
            IVt = [T("IV0"), T("IV1")]
            Dbt = [T("Db0"), T("Db1")]
            wpt = [T("wp0"), T("wp1")]
            stg_t = [T("pstg0"), T("pstg1")]
            for k in range(2):
                P.op("pool", lambda e, k=k: e.memset(Ub[k], 0.0), writes=[Ut[k]])
            mslabs = [A.alloc(8 * 512, BF16) for _ in range(2)]
            mslab_t = [T("mslabP0"), T("mslabP1")]
            ivt = A.alloc(192)
            ivt_t = T("ivt")
            P.dma("sp", ivt, ivt_d, writes=[ivt_t])
            tmp8 = A.alloc(8)
            tmp8_t = T("tmp8")
            npb = 0
            def p_loads(g):
                k = g % 2
                load_padded("sp", Ub[k], Ut[k], ZP[g * 128:(g + 1) * 128, :])
                P.dma("pool", wpb[k], wpool_d[l, g], writes=[wpt[k]])

            p_loads(0)
            for g in range(4):
                k = g % 2
                U, S1, S2, IV, Db, wp = Ub[k], S1b[k], S2b[k], IVb[k], Dbb[k], wpb[k]
                U_t, S1_t, S2_t, IV_t, Db_t, wp_t = Ut[k], S1t[k], S2t[k], IVt[k], Dbt[k], wpt[k]
                if g < 3:
                    p_loads(g + 1)
                cur, cur_t = U, U_t
                bufs = [(S1, S1_t), (S2, S2_t)]
                for lev in range(g + 1):
                    dst, dst_t = bufs[lev % 2]
                    if lev == 0:
                        lo, hi, sa, sb2 = 1, PADN - 1, -1, 0
                    else:
                        sh = 1 << (lev - 1)
                        lo, hi, sa, sb2 = (1 << lev), PADN - (1 << lev), -sh, sh
                    n = hi - lo
                    P.op("dve", lambda e, dst=dst, cur=cur, lo=lo, n=n, sa=sa, sb2=sb2: e.tensor_tensor(
                        out=dst[:, lo:lo + n], in0=cur[:, lo + sa:lo + sa + n], in1=cur[:, lo + sb2:lo + sb2 + n], op=ALU.add),
                        reads=[cur_t], writes=[dst_t])
                    cur, cur_t = dst, dst_t
                lo, n = 16, PADN - 32
                wv = float(1.0 / (2 << g))
                P.op("dve", lambda e, cur=cur, Db=Db, U=U, lo=lo, n=n, wv=wv: e.scalar_tensor_tensor(
                    out=Db[:, lo:lo + n], in0=cur[:, lo:lo + n], scalar=wv, in1=U[:, lo:lo + n], op0=ALU.mult, op1=ALU.subtract),
                    reads=[cur_t, U_t], writes=[Db_t])
                for bi, pos in enumerate(P_BND):
                    P.op("dve", lambda e, cur=cur, pos=pos, g=g, bi=bi: e.tensor_tensor(
                        out=tmp8, in0=cur[:, pos:pos + 8], in1=ivt[:, (g * 6 + bi) * 8:(g * 6 + bi) * 8 + 8], op=ALU.mult),
                        reads=[cur_t, ivt_t], writes=[tmp8_t])
                    P.op("dve", lambda e, Db=Db, U=U, pos=pos: e.tensor_tensor(
                        out=Db[:, pos:pos + 8], in0=tmp8, in1=U[:, pos:pos + 8], op=ALU.subtract),
                        reads=[tmp8_t, U_t], writes=[Db_t])
                sb_ = g % 2
                po, _ = PV[("pscale", l)]
                for i in range(NTILE):
                    pb = npb % 4
                    npb += 1
                    o_ps = ps[pb] if i < 8 else as3(ps[pb])
                    P.op("pe", lambda e, o_ps=o_ps, i=i, wp=wp, Db=Db: e.matmul(o_ps, lhsT=wp, rhs=ptile(Db, i), start=True, stop=True),
                         reads=[wp_t, Db_t], writes=[pst[pb]])
                    P.op("act", lambda e, pb=pb, i=i, sb_=sb_, g=g, po=po: e.activation(
                        out=stg[sb_][:, i * TT:(i + 1) * TT], in_=ps[pb], func=AF.Copy, scale=pv[:, po + g:po + g + 1]),
                        reads=[pst[pb], pv_t], writes=[stg_t[sb_]])
                P.dma("act", PO[g * 128:(g + 1) * 128, :], stg[sb_], reads=[stg_t[sb_]])
                if l == 0 and nl > 1:
                    m_slabs(1, range(3 * g, 3 * g + 3), mslabs, mslab_t)
                    if g == 3:
                        m_finish(1)

        def phase_Q(l):
            A.top = 0
            rC = A.alloc(NS)
            rS = A.alloc(NS)
            rope_t = [T("ropeC"), T("ropeS")]
            P.dma("sp", rC, ropeC_d, writes=[rope_t[0]])
            P.dma("sp", rS, ropeS_d, writes=[rope_t[1]])
            permf = A.alloc(128)
            perm_t = T("permf")
            P.dma("sp", permf, permf_d, writes=[perm_t])
            raw = [A.alloc(NT) for _ in range(2)]
            rawp = [A.alloc(NS) for _ in range(2)]
            raw_t = [T("raw0"), T("raw1")]
            rawp_t = [T("rawp0"), T("rawp1")]
            sqb = [A.alloc(NT, BF16) for _ in range(2)]
            sqb_t = [T("qsq0"), T("qsq1")]
            rstdb = [A.alloc(NT) for _ in range(2)]
            rstdb_t = [T("rstd0"), T("rstd1")]
            outb = [A.alloc(NT, BF16) for _ in range(2)]
            outb_t = [T("outb0"), T("outb1")]
            kf = A.alloc(512)
            kf_t = T("kf")
            kout = A.alloc(512)
            kout_t = T("kout")
            n = 0
            npb = 0
            for which, (src, dst, gname) in enumerate(((QR, QT, "qg"), (KR, KT, "kg"))):
                go, _ = PV[(gname, l)]
                gpo, _ = PV[(gname + "p", l)]
                gcol = pv[:, go:go + 1]
                gpcol = pv[:, gpo:gpo + 1]
                for c in range(4):
                    b = n % 2
                    n += 1
                    sq, sq_t, rstd, rstd_t = sqb[b], sqb_t[b], rstdb[b], rstdb_t[b]
                    v1, v2, v_t = raw[b][:, 0:NS], rawp[b], raw_t[b]
                    if n == 1:
                        P.dma("sp", raw[b], src[c * 128:(c + 1) * 128, :], writes=[raw_t[b]])
                    if n < 8:
                        nsrc = (QR, KR)[n // 4]
                        nc_ = n % 4
                        P.dma("sp", raw[1 - b], nsrc[nc_ * 128:(nc_ + 1) * 128, :], writes=[raw_t[1 - b]])

                    P.op("act", lambda e, b=b, sq=sq: e.activation(out=sq, in_=raw[b], func=AF.Square), reads=[raw_t[b]], writes=[sq_t])
                    for i in range(NTILE):
                        pb = npb % 4
                        npb += 1
                        P.op("pe", lambda e, pb=pb, i=i, sq=sq: e.matmul(ps[pb], lhsT=bones_b, rhs=sq[:, i * TT:(i + 1) * TT], start=True, stop=True),
                             reads=[sq_t, cst_t], writes=[pst[pb]])
                        P.op("act", lambda e, pb=pb, i=i, rstd=rstd: e.activation(out=rstd[:, i * TT:(i + 1) * TT], in_=ps[pb], func=AF.Ln,
                                                                      scale=1.0 / 64, bias=epsc),
                             reads=[pst[pb], cst_t], writes=[rstd_t])
                    P.op("act", lambda e, rstd=rstd: e.activation(out=rstd, in_=rstd, func=AF.Exp, scale=-0.5), reads=[rstd_t], writes=[rstd_t])
                    for i in range(8):
                        pb = npb % 4
                        npb += 1
                        P.op("pe", lambda e, pb=pb, i=i, b=b: e.matmul(ps[pb], lhsT=permf, rhs=raw[b][:, i * TT:(i + 1) * TT], start=True, stop=True),
                             reads=[raw_t[b], perm_t], writes=[pst[pb]])
                        P.op("dve", lambda e, pb=pb, i=i, gpcol=gpcol, v2=v2: e.scalar_tensor_tensor(
                            out=v2[:, i * TT:(i + 1) * TT], in0=ps[pb], scalar=gpcol, in1=rS[:, i * TT:(i + 1) * TT], op0=ALU.mult, op1=ALU.mult),
                            reads=[pst[pb], rope_t, pv_t], writes=[rawp_t[b]])
                    P.op("dve", lambda e, b=b, gcol=gcol, v1=v1: e.scalar_tensor_tensor(out=v1, in0=raw[b][:, 0:NS], scalar=gcol, in1=rC,
                                                                                       op0=ALU.mult, op1=ALU.mult),
                         reads=[rope_t, pv_t], writes=[raw_t[b]])
                    P.op("dve", lambda e, v1=v1, v2=v2: e.tensor_tensor(out=v1, in0=v1, in1=v2, op=ALU.add), reads=[rawp_t[b]], writes=[raw_t[b]])
                    P.op("dve", lambda e, b=b, v1=v1, rstd=rstd: e.tensor_tensor(out=outb[b][:, 0:NS], in0=v1, in1=rstd[:, 0:NS], op=ALU.mult),
                         reads=[raw_t[b], rstd_t], writes=[outb_t[b]])
                    P.op("dve", lambda e, b=b, gcol=gcol, rstd=rstd: e.scalar_tensor_tensor(out=outb[b][:, NS:NT], in0=raw[b][:, NS:NT], scalar=gcol,
                                                                                           in1=rstd[:, NS:NT], op0=ALU.mult, op1=ALU.mult),
                         reads=[raw_t[b], rstd_t, pv_t], writes=[outb_t[b]])
                    P.dma("pool", dst[c * 128:(c + 1) * 128, :], outb[b], reads=[outb_t[b]])
                    if which == 1:
                        P.op("dve", lambda e, b=b, gcol=gcol, rstd=rstd: e.scalar_tensor_tensor(out=kf, in0=raw[b][:, NS:NT], scalar=gcol,
                                                                                               in1=rstd[:, NS:NT], op0=ALU.mult, op1=ALU.mult),
                             reads=[raw_t[b], rstd_t, pv_t], writes=[kf_t])
                        pb = npb % 4
                        npb += 1
                        for tt in range(4):
                            P.op("pe", lambda e, pb=pb, tt=tt: e.transpose(ps[pb][:, tt * 128:(tt + 1) * 128], kf[:, tt * 128:(tt + 1) * 128], ident),
                                 reads=[kf_t, ident_t], writes=[pst[pb]], inc=(tt == 3))
                        P.op("act", lambda e, pb=pb: e.copy(out=kout, in_=ps[pb]), reads=[pst[pb]], writes=[kout_t])
                        for s_ in range(2):
                            P.dma("pool", nk_d[s_, l, :, c * 128:(c + 1) * 128].rearrange("(h p) f -> p h f", p=128),
                                  V(kout, s_ * 256, [(128, 2), (1, 128)]), reads=[kout_t])

        def phase_L(l):
            A.top = 0
            U = A.alloc(PADN)
            XC = A.alloc(PADN)
            XCb = A.alloc(PADN, BF16)
            Rb = [A.alloc(NT), A.alloc(NT)]
            Ib = [A.alloc(NT), A.alloc(NT)]
            TMb = [A.alloc(NT), A.alloc(NT)]
            H = TMb
            UGbb = [A.alloc(NT, BF16) for _ in range(2)]
            ob = A.alloc(NT, BF16)
            wbdb = [[A.alloc(128, BF16) for _ in range(4)] for _ in range(2)]
            U_t, XC_t, XCb_t, ob_t = (T("lU"), T("lXC"), T("lXCb"), T("lob"))
            UGt_ = [T("lUG0"), T("lUG1")]
            wbdt_ = [T("lwbd0"), T("lwbd1")]
            Rt_ = [T("lR0"), T("lR1")]
            It_ = [T("lI0"), T("lI1")]
            TMt_ = [T("lTM0"), T("lTM1")]
            H_t = TMt_
            P.op("dve", lambda e: e.memset(U, 0.0), writes=[U_t])
            cwo, _ = PV[("convw", l)]
            cbo, _ = PV[("convb", l)]
            bao, _ = PV[("ba", l)]
            bio, _ = PV[("bi", l)]
            h0o, _ = PV[("h0", l)]
            npb = 0

            def rev(ap):
                (ps_, pn), (fs, fn) = ap.ap
                return bass.AP(ap.tensor, ap.offset + (fn - 1) * fs, [[ps_, pn], [-fs, fn]])

            def l_loads(c):
                load_padded("sp", U, U_t, UL[c * 128:(c + 1) * 128, :])
                P.dma("sp", UGbb[c % 2], UG[c * 128:(c + 1) * 128, :], writes=[UGt_[c % 2]])
                for d in range(2):
                    for ai in range(2):
                        P.dma("pool", wbdb[c % 2][d * 2 + ai], lbd_d[l, d, ai, c], writes=[wbdt_[c % 2]])

            l_loads(0)
            for c in range(4):
                UGb, UG_t = UGbb[c % 2], UGt_[c % 2]
                wbd, wbd_t = wbdb[c % 2], wbdt_[c % 2]
                lo, n = 1, PADN - 3
                P.op("act", lambda e, c=c, lo=lo, n=n: e.activation(out=XC[:, lo:lo + n], in_=U[:, lo - 1:lo - 1 + n], func=AF.Identity,
                                                                    scale=pv[:, cwo + c:cwo + c + 1], bias=pv[:, cbo + c:cbo + c + 1]),
                     reads=[U_t, pv_t], writes=[XC_t])
                for j in range(1, 4):
                    P.op("dve", lambda e, c=c, j=j, lo=lo, n=n: e.scalar_tensor_tensor(
                        out=XC[:, lo:lo + n], in0=U[:, lo - 1 + j:lo - 1 + j + n], scalar=pv[:, cwo + j * 4 + c:cwo + j * 4 + c + 1],
                        in1=XC[:, lo:lo + n], op0=ALU.mult, op1=ALU.add), reads=[U_t, XC_t, pv_t], writes=[XC_t])
                P.op("dve", lambda e, lo=lo, n=n: e.tensor_copy(out=XCb[:, lo:lo + n], in_=XC[:, lo:lo + n]), reads=[XC_t], writes=[XCb_t])
                if c < 3:
                    l_loads(c + 1)
                for d in range(2):
                    R, I, TM = Rb[d], Ib[d], TMb[d]
                    R_t, I_t, TM_t = Rt_[d], It_[d], TMt_[d]
                    for i in range(NTILE):
                        for ai, (dstb, dst_t, bo) in enumerate(((R, R_t, bao), (I, I_t, bio))):
                            pb = npb % 4
                            npb += 1
                            o_ps = ps[pb] if i < 8 else as3(ps[pb])
                            P.op("pe", lambda e, o_ps=o_ps, i=i, d=d, ai=ai, wbd=wbd: e.matmul(o_ps, lhsT=wbd[d * 2 + ai], rhs=ptile(XCb, i),
                                                                                      start=True, stop=True),
                                 reads=[wbd_t, XCb_t], writes=[pst[pb]])
                            P.op("act", lambda e, pb=pb, i=i, dstb=dstb, bo=bo, d=d, c=c: e.activation(
                                out=dstb[:, i * TT:(i + 1) * TT], in_=ps[pb], func=AF.Sigmoid, bias=pv[:, bo + d * 4 + c:bo + d * 4 + c + 1]),
                                reads=[pst[pb], pv_t], writes=[dst_t])
                    P.op("act", lambda e, d=d, c=c, R=R: e.activation(out=R, in_=R, func=AF.Exp, scale=cl8[l][:, d * 4 + c:d * 4 + c + 1]),
                         reads=[R_t, mod_t[l]], writes=[R_t])
                    P.op("act", lambda e, R=R, TM=TM: e.activation(out=TM, in_=R, func=AF.Square), reads=[R_t], writes=[TM_t])
                    P.op("act", lambda e, TM=TM: e.activation(out=TM, in_=TM, func=AF.Sqrt, scale=-1.0, bias=onec), reads=[TM_t, cst_t], writes=[TM_t])
                    P.op("dve", lambda e, I=I, TM=TM: e.tensor_tensor(out=I, in0=I, in1=TM, op=ALU.mult), reads=[I_t, TM_t], writes=[I_t])
                    P.op("dve", lambda e, I=I: e.tensor_tensor(out=I[:, 0:NS], in0=I[:, 0:NS], in1=XC[:, P_S0:P_S0 + NS], op=ALU.mult),
                         reads=[I_t, XC_t], writes=[I_t])
                    P.op("dve", lambda e, I=I: e.tensor_tensor(out=as3(I[:, NS:NT]), in0=as3(I[:, NS:NT]), in1=ptile(XC, 8), op=ALU.mult),
                         reads=[I_t, XC_t], writes=[I_t])
                    Hd = H[d]
                    segs = [(0, NS, pv[:, h0o + d * 4 + c:h0o + d * 4 + c + 1]), (NS, 256, 0.0), (NS + 256, 256, 0.0)]
                    for (s0, sn, init) in segs:
                        if d == 0:
                            P.op("dve", lambda e, Hd=Hd, s0=s0, sn=sn, init=init, R=R, I=I: e.tensor_tensor_scan(
                                out=Hd[:, s0:s0 + sn], data0=R[:, s0:s0 + sn], data1=I[:, s0:s0 + sn], initial=init, op0=ALU.mult, op1=ALU.add),
                                reads=[R_t, I_t, pv_t], writes=[H_t[d]])
                        else:
                            P.op("dve", lambda e, Hd=Hd, s0=s0, sn=sn, init=init, R=R, I=I: e.tensor_tensor_scan(
                                out=rev(Hd[:, s0:s0 + sn]), data0=rev(R[:, s0:s0 + sn]), data1=rev(I[:, s0:s0 + sn]), initial=init,
                                op0=ALU.mult, op1=ALU.add), reads=[R_t, I_t, pv_t], writes=[H_t[d]])
                    for s_ in range(2):
                        col = ((s_ * NL + l) * 2 + d) * 4 + c
                        srcc = NS + s_ * 256 + (255 if d == 0 else 0)
                        P.op("pool", lambda e, Hd=Hd, col=col, srcc=srcc: e.tensor_copy(out=fin[:, col:col + 1], in_=Hd[:, srcc:srcc + 1]),
                             reads=[H_t[d]], writes=[fin_t])
                P.op("dve", lambda e: e.tensor_tensor(out=H[0], in0=H[0], in1=H[1], op=ALU.add), reads=[H_t[0], H_t[1]], writes=[H_t[0]])
                P.op("dve", lambda e, UGb=UGb: e.tensor_tensor(out=ob, in0=H[0], in1=UGb, op=ALU.mult), reads=[H_t[0], UG_t], writes=[ob_t])
                P.dma("sp", LO[c * 128:(c + 1) * 128, :], ob, reads=[ob_t])

        def write_fin():
            P.op("pe", lambda e: e.transpose(ps[7][0:32, 0:128], fin, ident), reads=[fin_t, ident_t], writes=[pst[7]])
            fo = PA.alloc(128)
            fo_t = T("fo")
            P.op("act", lambda e: e.copy(out=fo[0:32, :], in_=ps[7][0:32, 0:128]), reads=[pst[7]], writes=[fo_t])
            P.dma("sp", nh_d, fo[0:32, :], reads=[fo_t])

        def phase_T(l):
            A.top = 0
            Vs = A.alloc(32 * 520 + 64, BF16)
            Vc = A.alloc(4 * 520 + 64, BF16)
            Vp = A.alloc(4 * 520 + 64, BF16)
            V_t = [T("Vt%d" % i) for i in range(14)]
            P.op("pool", lambda e: e.memset(Vc, 1.0), writes=[V_t[5:13]])
            P.op("pool", lambda e: e.memset(Vs[:, 32 * 520:32 * 520 + 64], 1.0), writes=[V_t[13]])
            P.op("pool", lambda e: e.memset(Vp[:, 4 * 520:4 * 520 + 64], 1.0), writes=[V_t[13]])
            for g4 in range(4):
                P.dma("sp", V(Vs, g4 * 8 * 520, [(520, 8), (1, 520)]),
                      VS[g4 * 1024:(g4 + 1) * 1024, :].rearrange("(pr p) n -> p pr n", p=128), writes=[V_t[g4]])
            P.dma("sp", V(Vp, 0, [(520, 4), (1, 520)]), VS[NS:NT, :].rearrange("(pr p) n -> p pr n", p=128), writes=[V_t[4]])
            for h in range(8):
                P.dma("pool", V(Vc, h * 65, [(520, 4), (1, 64)]),
                      cv_d[l][:, h * 64:(h + 1) * 64].rearrange("(kc p) d -> p kc d", p=128), writes=[V_t[5 + h]])
            ckt = A.alloc(4 * 512)
            ckt_t = T("ckt")
            P.dma("sp", V(ckt, 0, [(512, 4), (1, 512)]), ck_d[l].rearrange("(kc p) n -> p kc n", p=128), writes=[ckt_t])
            KcTz = [[A.alloc(512, BF16) for _ in range(4)] for _ in range(2)]
            KcT_t = T("KcT")
            for hp in range(2):
                for c in range(4):
                    P.op("pool", lambda e, hp=hp, c=c: e.memset(KcTz[hp][c], 0.0), writes=[KcT_t])
            for c in range(4):
                for kc in range(4):
                    P.op("pe", lambda e, c=c, kc=kc: e.transpose(ps[c][:, kc * 128:(kc + 1) * 128],
                                                               ckt[:, kc * 512 + c * 128: kc * 512 + c * 128 + 128], ident),
                         reads=[ckt_t, ident_t], writes=[pst[c]], inc=(kc == 3))
                P.op("act", lambda e, c=c: e.copy(out=KcTz[0][c][0:64, :], in_=ps[c][0:64, :]), writes=[pst[c], KcT_t])
                P.op("act", lambda e, c=c: e.copy(out=KcTz[1][c][64:128, :], in_=ps[c][64:128, :]), writes=[pst[c], KcT_t])
            Mint = A.alloc(NY * 64, BF16)
            Medge = A.alloc(12 * 512, BF16)
            M_t = [T("masks0"), T("masks1"), T("masks2")]
            P.dma("pool", Mint, mint_d, writes=[M_t[0]])
            for mh in range(2):
                P.dma("pool", V(Medge, mh * 6 * 512, [(512, 6), (1, 512)]), medge_d[mh * 6:(mh + 1) * 6].rearrange("j p n -> p j n"), writes=[M_t[1 + mh]])
            import os as _os2
            KTS = int(_os2.environ.get("KTS", "9"))
            if KTS <= 1:
                return
            Gh = [A.alloc(NY * 64) for _ in range(2)]
            Tint = [A.alloc(NY * 64, BF16) for _ in range(2)]
            Tedge = [A.alloc(12 * 512, BF16) for _ in range(2)]
            Gh_t = [[T("Gh%d_%d" % (a_, b_)) for b_ in range(4)] for a_ in range(2)]
            Tb_t = [T("Tb0"), T("Tb1")]
            QTc = [A.alloc(NT, BF16) for _ in range(2)]
            KTz = [[A.alloc(NT, BF16) for _ in range(2)] for _ in range(2)]
            QK_t = [T("QK0"), T("QK1")]
            for cb_ in range(2):
                P.op("pool", lambda e, cb_=cb_: e.memset(KTz[cb_][0][64:128, :], 0.0), writes=[QK_t[cb_]])
                P.op("pool", lambda e, cb_=cb_: e.memset(KTz[cb_][1][0:64, :], 0.0), writes=[QK_t[cb_]])
            Sb = [A.alloc(512) for _ in range(2)]
            Sb_t = [T("Sb0"), T("Sb1")]
            Pt = [A.alloc(512, BF16) for _ in range(5)]
            Pt_t = [T("Pt%d" % i) for i in range(5)]
            denrow = A.alloc(512)
            rden = A.alloc(512)
            den_t, rden_t = T("den"), T("rden")
            stage = [A.alloc(NT, BF16) for _ in range(2)]
            stage_t = [T("ast0"), T("ast1")]
            cnt = {"s": 0, "p": 0, "q": 0, "b": 0}

            def finalize(po, n, out_ap, out_t):
                P.op("act", lambda e: e.copy(out=denrow[64:65, 0:n], in_=ps[po][64:65, 0:n]), writes=[pst[po], den_t])
                P.op("pe", lambda e: e.matmul(ps[6][0:64, 0:n], lhsT=ones_f[64:65, 0:64], rhs=denrow[64:65, 0:n], start=True, stop=True),
                     reads=[den_t, cst_t], writes=[pst[6]])
                P.op("dve", lambda e: e.reciprocal(out=rden[0:64, 0:n], in_=ps[6][0:64, 0:n]), reads=[pst[6]], writes=[rden_t])
                P.op("dve", lambda e: e.tensor_tensor(out=out_ap, in0=ps[po][0:64, 0:n], in1=rden[0:64, 0:n], op=ALU.mult),
                     reads=[rden_t], writes=[pst[po], out_t])

            LOOK = 4
            pres = {}
            early_idx = {}
            head_first = {}
            SBANKS = [0, 1, 2, 3, 7]
            NSB = 5
            NPT = 5
            tiles = []
            for h in range(8):
                c = h // 2
                hb = (h % 2) * 64
                tb = h % 2
                cb = c % 2

                def pre(h=h, c=c, tb=tb, cb=cb):
                    if h % 2 == 0:
                        P.dma("sp", QTc[cb], QT[c * 128:(c + 1) * 128, :], writes=[QK_t[cb]])
                        P.dma("sp", KTz[cb][0][0:64, :], KT[c * 128:c * 128 + 64, :], writes=[QK_t[cb]])
                        P.dma("sp", KTz[cb][1][64:128, :], KT[c * 128 + 64:(c + 1) * 128, :], writes=[QK_t[cb]])
                    for half in range(2):
                        for yh in range(2):
                            src = bass.AP(rbp_d.tensor, rbp_d.offset + l * RB_LEN + RB_PAD + h * 465 + (-3 - half + yh * 11) * 31 - 48,
                                          [[1, 64], [31, 11], [1, 64]])
                            P.dma("sp", V(Gh[tb], yh * 11 * 64, [(64, 11), (1, 64)], p0=half * 64, pn=64), src, writes=[Gh_t[tb][half * 2 + yh]])
                    P.op("dve", lambda e: e.scalar_tensor_tensor(out=V(Tint[tb], 0, [(64, NY), (1, 64)]), in0=V(Gh[tb], 63, [(64, NY), (-1, 64)]),
                                                                 scalar=8.0, in1=V(Mint, 0, [(64, NY), (1, 64)]), op0=ALU.mult, op1=ALU.add),
                         reads=[Gh_t[tb], M_t], writes=[Tb_t[tb]])
                    for blk, (rq0, krs) in enumerate(((0, TOP_KR), (56, BOT_KR))):
                        for j, kr in enumerate(krs):
                            y0 = E_MAX - (kr - rq0)
                            jj = blk * 6 + j
                            P.op("dve", lambda e, y0=y0, jj=jj: e.scalar_tensor_tensor(
                                out=V(Tedge[tb], jj * 512, [(64, 8), (1, 64)]), in0=V(Gh[tb], y0 * 64 + 63, [(64, 8), (-1, 64)]),
                                scalar=8.0, in1=V(Medge, jj * 512, [(64, 8), (1, 64)]), op0=ALU.mult, op1=ALU.add),
                                reads=[Gh_t[tb], M_t], writes=[Tb_t[tb]])

                pres[h] = pre
                head_first[h] = len(tiles)
                first_of_head = (h == 0)
                for qb in range(8):
                    rq0 = 8 * qb
                    early_mark = len(tiles) if qb == 3 else None
                    if early_mark is not None:
                        early_idx[h] = early_mark
                    if qb == 0:
                        chunks = [(kr, Tedge[tb][:, j * 512:(j + 1) * 512]) for j, kr in enumerate(TOP_KR)]
                    elif qb == 7:
                        chunks = [(kr, Tedge[tb][:, (6 + j) * 512:(7 + j) * 512]) for j, kr in enumerate(BOT_KR)]
                    else:
                        chunks = []
                        for kr in range(rq0 - 4, rq0 + 12, 2):
                            y0 = E_MAX - (kr - rq0)
                            chunks.append((kr, Tint[tb][:, y0 * 64:(y0 + 8) * 64]))
                    q_ap = QTc[cb][:, rq0 * 64: rq0 * 64 + 512]
                    po = 4 + cnt["q"] % 2
                    cnt["q"] += 1
                    ntot = len(chunks) + 4
                    k = 0
                    for kc in range(4):
                        tiles.append(dict(pre=pre if first_of_head else None, n=512, c0=0, q=q_ap, tb=tb, cb=cb,
                                          kT=KcTz[h % 2][c][:, kc * 128:(kc + 1) * 128], kT_t=[QK_t[cb], KcT_t], tbl=None,
                                          vl=V(Vc, kc * 520 + h * 65, [(1, 128)]), po=po, first=(k == 0), last=(k == ntot - 1),
                                          out=stage[tb][0:64, rq0 * 64: rq0 * 64 + 512], post=None))
                        first_of_head = False
                        k += 1
                    for (kr, tbl) in chunks:
                        rows = [r for r in range(8) if any(_r0(rq0 + r) <= kr + hf <= _r0(rq0 + r) + 7 for hf in range(2))]
                        ra, rb = min(rows), max(rows)
                        assert rows == list(range(ra, rb + 1))
                        c0, n = ra * 64, (rb - ra + 1) * 64
                        tiles.append(dict(pre=None, n=n, c0=c0, q=q_ap[:, c0:c0 + n], tb=tb, cb=cb,
                                          kT=KTz[cb][h % 2][:, kr * 64: kr * 64 + 128], kT_t=[QK_t[cb]], tbl=tbl[:, c0:c0 + n],
                                          vl=V(Vs, (kr // 2) * 520 + h * 65, [(1, 128)]), po=po, first=(k == 0), last=(k == ntot - 1),
                                          out=stage[tb][0:64, rq0 * 64: rq0 * 64 + 512], post=None))
                        k += 1
                for s_ in range(2):
                    q_ap = QTc[cb][:, NS + s_ * 256: NS + s_ * 256 + 256]
                    po = 4 + cnt["q"] % 2
                    cnt["q"] += 1
                    for kc in range(2):
                        k0 = NS + s_ * 256 + kc * 128
                        post = None
                        if s_ == 1 and kc == 1:
                            def post(h=h, tb=tb):
                                P.dma("sp", AO[h * 64:(h + 1) * 64, :], stage[tb][0:64, :], reads=[stage_t[tb]])
                        tiles.append(dict(pre=None, n=256, c0=0, q=q_ap, tb=tb, cb=cb,
                                          kT=KTz[cb][h % 2][:, k0:k0 + 128], kT_t=[QK_t[cb]], tbl=None,
                                          vl=V(Vp, (s_ * 2 + kc) * 520 + h * 65, [(1, 128)]), po=po, first=(kc == 0), last=(kc == 1),
                                          out=stage[tb][0:64, NS + s_ * 256: NS + s_ * 256 + 256], post=post))

            deferred = []

            def fin_a(t):
                po, n = t["po"], t["gn"]
                P.op("act", lambda e: e.copy(out=denrow[64:65, 0:n], in_=ps[po][64:65, 0:n]), writes=[pst[po], den_t])

            def fin_b(t):
                po, n, out_ap, tb = t["po"], t["gn"], t["out"], t["tb"]
                P.op("pe", lambda e: e.matmul(ps[6][0:64, 0:n], lhsT=ones_f[64:65, 0:64], rhs=denrow[64:65, 0:n], start=True, stop=True),
                     reads=[den_t, cst_t], writes=[pst[6]])
                P.op("act", lambda e: e.activation(out=rden[0:64, 0:n], in_=ps[6][0:64, 0:n], func=AF.Ln), reads=[pst[6]], writes=[rden_t])
                P.op("act", lambda e: e.activation(out=rden[0:64, 0:n], in_=rden[0:64, 0:n], func=AF.Exp, scale=-1.0), reads=[rden_t], writes=[rden_t])
                P.op("dve", lambda e: e.tensor_tensor(out=out_ap, in0=ps[po][0:64, 0:n], in1=rden[0:64, 0:n], op=ALU.mult),
                     reads=[rden_t], writes=[pst[po], stage_t[tb]])
                if t["post"] is not None:
                    t["post"]()

            for h_ in range(7):
                k_ = head_first[h_ + 1] - 8
                assert tiles[k_]["pre"] is None
                tiles[k_]["pre"] = pres[h_ + 1]
            NTL = len(tiles)
            for t in tiles:
                t["gn"] = 256 if t["vl"].tensor is Vp.tensor and False else None
            gw = None
            for t in tiles:
                if t["first"]:
                    gw = t["n"]
                t["gn"] = gw
            for idx in range(NTL + LOOK + 3):
                while deferred and deferred[0][0] <= idx:
                    fin_b(deferred.pop(0)[1])
                if idx < NTL:
                    t = tiles[idx]
                    if t["pre"] is not None:
                        t["pre"]()
                    sb_ = SBANKS[idx % NSB]
                    n = t["n"]
                    loc = t["tbl"] is not None
                    P.op("pe", lambda e, t=t, sb_=sb_, n=n, loc=loc: e.matmul(ps[sb_][:, 0:n], lhsT=t["kT"], rhs=t["q"], start=True, stop=not loc),
                         reads=t["kT_t"], writes=[pst[sb_]], inc=not loc)
                    if loc:
                        P.op("pe", lambda e, t=t, sb_=sb_, n=n: e.matmul(ps[sb_][:, 0:n], lhsT=ident_b, rhs=t["tbl"], start=False, stop=True),
                             reads=[cst_t, Tb_t[t["tb"]]], writes=[pst[sb_]])
                j = idx - LOOK
                if 0 <= j < NTL:
                    t = tiles[j]
                    sb_ = SBANKS[j % NSB]
                    p_ = j % NPT
                    n = t["n"]
                    P.op("act", lambda e, sb_=sb_, p_=p_, n=n: e.activation(out=Pt[p_][:, 0:n], in_=ps[sb_][:, 0:n], func=AF.Exp, scale=0.125),
                         reads=[pst[sb_]], writes=[Pt_t[p_]])
                    P.op("pe", lambda e, t=t, p_=p_, n=n: e.matmul(ps[t["po"]][:, t["c0"]:t["c0"] + n], lhsT=t["vl"], rhs=Pt[p_][:, 0:n],
                                                                  start=t["first"], stop=t["last"]),
                         reads=[V_t, Pt_t[p_]], writes=[pst[t["po"]]])
                    if t["last"]:
                        fin_a(t)
                        deferred.append((idx + 2, t))
            assert not deferred

        def phase_C1(l):
            A.top = 0
            wbr = A.alloc(3 * 4096, BF16)
            wo = A.alloc(8 * 1024, BF16)
            w_t = [T("c1w%d" % i) for i in range(5)]
            for b in range(3):
                P.dma("pool", V(wbr, b * 4096, [(1024, 4), (1, 1024)]), wbr_d[l, b].rearrange("(kc p) n -> p kc n", p=128), writes=[w_t[b]])
            for hf in range(2):
                P.dma("pool", V(wo, hf * 4096, [(1024, 4), (1, 1024)]),
                      wout_d[l, hf * 512:(hf + 1) * 512, :].rearrange("(kc p) n -> p kc n", p=128), writes=[w_t[3 + hf]])
            Bt = [[A.alloc(4 * TT, BF16) for _ in range(3)] for _ in range(2)]
            Bt_t = [[T("Bt%d%d" % (a_, b)) for b in range(3)] for a_ in range(2)]
            xT3 = [A.alloc(8 * TT) for _ in range(3)]
            xT3_t = [T("c1x0"), T("c1x1"), T("c1x2")]
            Gt = [A.alloc(3 * TT, BF16) for _ in range(3)]
            Gt_t = [T("Gt0"), T("Gt1"), T("Gt2")]
            mg = [A.alloc(8 * TT, BF16) for _ in range(2)]
            mg_t = [T("mg0"), T("mg1")]
            tmpb = [[A.alloc(TT) for _ in range(3)] for _ in range(2)]
            tmpb_t = [[T("tmp%d%d" % (a_, b)) for b in range(3)] for a_ in range(2)]
            accb = [A.alloc(TT) for _ in range(2)]
            accb_t = [T("acc0"), T("acc1")]
            sq = A.alloc(8 * TT, BF16)
            sq_t = T("c1sq")
            rs = A.alloc(TT)
            rs_t = T("c1rs")
            xn = A.alloc(8 * TT)
            xn_t = T("c1xn")
            h2s = [A.alloc(8 * TT, BF16) for _ in range(2)]
            h2s_t = [T("h2s0"), T("h2s1")]
            st = {"gn": 0, "pbn": 0}
            GSr = GS.rearrange("(b r) t -> r b t", b=3)

            def loads_B(i):
                b_ = i % 2
                for bi, src in enumerate((PO, AO, LO)):
                    P.dma("sp", V(Bt[b_][bi], 0, [(TT, 4), (1, TT)]), src[:, i * TT:(i + 1) * TT].rearrange("(kc p) t -> p kc t", p=128),
                          writes=[Bt_t[b_][bi]])

            def load_x(i):
                xi = i % 3
                P.dma("sp", V(xT3[xi], 0, [(TT, 8), (1, TT)]), XT[:, i * TT:(i + 1) * TT].rearrange("(c p) t -> p c t", p=128),
                      writes=[xT3_t[xi]])

            def branch(i):
                b_ = i % 2
                for m in range(8):
                    g_ = st["gn"] % 3
                    st["gn"] += 1
                    P.dma("sp", V(Gt[g_], 0, [(TT, 3), (1, TT)]), GSr[m * 128:(m + 1) * 128, :, i * TT:(i + 1) * TT], writes=[Gt_t[g_]])
                    tmp, tmp_t, acc, acc_t = tmpb[m % 2], tmpb_t[m % 2], accb[m % 2], accb_t[m % 2]
                    for bi in range(3):
                        pb = st["pbn"] % 5
                        st["pbn"] += 1
                        for kc in range(4):
                            P.op("pe", lambda e, pb=pb, bi=bi, kc=kc, m=m, b_=b_: e.matmul(
                                ps[pb], lhsT=wbr[:, bi * 4096 + kc * 1024 + m * 128: bi * 4096 + kc * 1024 + m * 128 + 128],
                                rhs=Bt[b_][bi][:, kc * TT:(kc + 1) * TT], start=(kc == 0), stop=(kc == 3)),
                                reads=[w_t[bi], Bt_t[b_][bi]], writes=[pst[pb]], inc=(kc == 3))
                        P.op("dve", lambda e, pb=pb, bi=bi, g_=g_, tmp=tmp: e.tensor_tensor(out=tmp[bi], in0=ps[pb], in1=Gt[g_][:, bi * TT:(bi + 1) * TT],
                                                                                            op=ALU.mult),
                             reads=[pst[pb], Gt_t[g_]], writes=[tmp_t[bi]])
                    P.op("pool", lambda e, tmp=tmp, acc=acc: e.tensor_tensor(out=acc, in0=tmp[0], in1=tmp[1], op=ALU.add),
                         reads=[tmp_t[0], tmp_t[1]], writes=[acc_t])
                    P.op("pool", lambda e, m=m, b_=b_, tmp=tmp, acc=acc: e.tensor_tensor(out=mg[b_][:, m * TT:(m + 1) * TT], in0=acc, in1=tmp[2], op=ALU.add),
                         reads=[acc_t, tmp_t[2]], writes=[mg_t[b_]])

            def outproj(i):
                b_ = i % 2
                xi = i % 3
                j = 0 if i < 8 else 1
                for m2 in range(8):
                    pb = 5 + m2 % 2
                    for kc in range(8):
                        P.op("pe", lambda e, pb=pb, kc=kc, m2=m2, b_=b_: e.matmul(
                            ps[pb], lhsT=wo[:, kc * 1024 + m2 * 128: kc * 1024 + m2 * 128 + 128], rhs=mg[b_][:, kc * TT:(kc + 1) * TT],
                            start=(kc == 0), stop=(kc == 7)), reads=[w_t[3 + kc // 4], mg_t[b_]], writes=[pst[pb]], inc=(kc == 7))
                    P.op("dve", lambda e, xi=xi, pb=pb, m2=m2, j=j: e.scalar_tensor_tensor(
                        out=xT3[xi][:, m2 * TT:(m2 + 1) * TT], in0=ps[pb], scalar=modcol(l, j, 2, m2), in1=xT3[xi][:, m2 * TT:(m2 + 1) * TT],
                        op0=ALU.mult, op1=ALU.add), reads=[pst[pb], xT3_t[xi], mod_t[l]], writes=[xT3_t[xi]])
                P.dma("act", XT[:, i * TT:(i + 1) * TT].rearrange("(c p) t -> p c t", p=128), V(xT3[xi], 0, [(TT, 8), (1, TT)]), reads=[xT3_t[xi]])

            def norm2_part(i):
                b_ = i % 2
                xi = i % 3
                j = 0 if i < 8 else 1
                pn = 7
                P.op("act", lambda e, xi=xi: e.activation(out=sq, in_=xT3[xi], func=AF.Square), reads=[xT3_t[xi]], writes=[sq_t])
                for c in range(8):
                    P.op("pe", lambda e, c=c, pn=pn: e.matmul(ps[pn], lhsT=ones_b, rhs=sq[:, c * TT:(c + 1) * TT], start=(c == 0), stop=(c == 7)),
                         reads=[sq_t, cst_t], writes=[pst[pn]], inc=(c == 7))
                P.op("act", lambda e, pn=pn: e.activation(out=rs, in_=ps[pn], func=AF.Ln, scale=1.0 / D, bias=epsc),
                     reads=[pst[pn], cst_t], writes=[rs_t])
                P.op("act", lambda e: e.activation(out=rs, in_=rs, func=AF.Exp, scale=-0.5), reads=[rs_t], writes=[rs_t])
                P.op("dve", lambda e, xi=xi: e.tensor_tensor(out=V(xn, 0, [(TT, 8), (1, TT)]), in0=V(xT3[xi], 0, [(TT, 8), (1, TT)]),
                                                             in1=V(rs, 0, [(0, 8), (1, TT)]), op=ALU.mult),
                     reads=[rs_t, xT3_t[xi]], writes=[xn_t])
                for c in range(8):
                    P.op("act", lambda e, c=c, b_=b_, j=j: e.activation(
                        out=h2s[b_][:, c * TT:(c + 1) * TT], in_=xn[:, c * TT:(c + 1) * TT], func=AF.Identity,
                        scale=A2[l][:, j * 8 + c: j * 8 + c + 1], bias=modcol(l, j, 3, c)),
                        reads=[xn_t, mod_t[l]], writes=[h2s_t[b_]])
                P.dma("act", H2[:, i * TT:(i + 1) * TT].rearrange("(c p) t -> p c t", p=128), V(h2s[b_], 0, [(TT, 8), (1, TT)]), reads=[h2s_t[b_]])

            loads_B(0)
            load_x(0)
            for i in range(NTILE + 2):
                if i + 1 < NTILE:
                    loads_B(i + 1)
                if i < NTILE:
                    branch(i)
                if 0 <= i - 2 < NTILE:
                    norm2_part(i - 2)
                if i + 1 < NTILE:
                    load_x(i + 1)
                if 0 <= i - 1 < NTILE:
                    outproj(i - 1)

        def phase_C2(l, last):
            A.top = 0
            wup = A.alloc(8 * 4096, BF16)
            wdn = A.alloc(32 * 1024, BF16)
            wu_t = [T("c2wu%d" % i) for i in range(4)]
            wd_t = [T("c2wd%d" % i) for i in range(4)]
            w_t = wu_t
            for q in range(4):
                P.dma("pool", V(wup, q * 1024, [(4096, 8), (1, 1024)]),
                      wup_d[l, :, q * 1024:(q + 1) * 1024].rearrange("(kc p) n -> p kc n", p=128), writes=[wu_t[q]])
            for q in range(4):
                P.dma("pool", V(wdn, q * 8192, [(1024, 8), (1, 1024)]),
                      wdn_d[l, q * 1024:(q + 1) * 1024, :].rearrange("(fc p) n -> p fc n", p=128), writes=[wd_t[q]])
            h2tb = [A.alloc(8 * TT, BF16) for _ in range(2)]
            x1t = A.alloc(8 * TT)
            a = A.alloc(32 * TT, BF16)
            h2tt = [T("h2t0"), T("h2t1")]
            x1t_t = T("x1t")
            a_t = [T("a%d" % f) for f in range(32)]
            yo = A.alloc(1024) if last else None
            yo_t = T("yo")
            P.dma("sp", V(h2tb[0], 0, [(TT, 8), (1, TT)]), H2[:, 0:TT].rearrange("(c p) t -> p c t", p=128), writes=[h2tt[0]])
            for i in range(NTILE):
                j = 0 if i < 8 else 1
                h2t, h2t_t = h2tb[i % 2], h2tt[i % 2]
                P.dma("sp", V(x1t, 0, [(TT, 8), (1, TT)]), XT[:, i * TT:(i + 1) * TT].rearrange("(c p) t -> p c t", p=128), writes=[x1t_t])
                if i + 1 < NTILE:
                    P.dma("sp", V(h2tb[(i + 1) % 2], 0, [(TT, 8), (1, TT)]), H2[:, (i + 1) * TT:(i + 2) * TT].rearrange("(c p) t -> p c t", p=128),
                          writes=[h2tt[(i + 1) % 2]])
                for f in range(32):
                    pb = f % 4
                    for kc in range(8):
                        P.op("pe", lambda e, pb=pb, f=f, kc=kc, h2t=h2t: e.matmul(
                            ps[pb], lhsT=wup[:, kc * 4096 + f * 128: kc * 4096 + f * 128 + 128], rhs=h2t[:, kc * TT:(kc + 1) * TT],
                            start=(kc == 0), stop=(kc == 7)), reads=[wu_t[f // 8], h2t_t], writes=[pst[pb]], inc=(kc == 7))
                    P.op("act", lambda e, pb=pb, f=f: e.activation(out=a[:, f * TT:(f + 1) * TT], in_=ps[pb], func=AF.Relu),
                         reads=[pst[pb]], writes=[a_t[f]])
                    eng = "dve" if f % 2 == 0 else "pool"
                    P.op(eng, lambda e, f=f: e.tensor_tensor(out=a[:, f * TT:(f + 1) * TT], in0=a[:, f * TT:(f + 1) * TT],
                                                             in1=a[:, f * TT:(f + 1) * TT], op=ALU.mult),
                         reads=[a_t[f]], writes=[a_t[f]])
                for m in range(8):
                    pb = 4 + m % 2
                    for fc in range(32):
                        P.op("pe", lambda e, pb=pb, fc=fc, m=m: e.matmul(
                            ps[pb], lhsT=wdn[:, fc * 1024 + m * 128: fc * 1024 + m * 128 + 128], rhs=a[:, fc * TT:(fc + 1) * TT],
                            start=(fc == 0), stop=(fc == 31)), reads=[wd_t[fc // 8], a_t[fc]], writes=[pst[pb]], inc=(fc == 31))
                    P.op("dve", lambda e, pb=pb, m=m, j=j: e.scalar_tensor_tensor(
                        out=x1t[:, m * TT:(m + 1) * TT], in0=ps[pb], scalar=modcol(l, j, 5, m), in1=x1t[:, m * TT:(m + 1) * TT],
                        op0=ALU.mult, op1=ALU.add), reads=[pst[pb], x1t_t, mod_t[l]], writes=[x1t_t])
                if not last:
                    P.dma("sp", XT[:, i * TT:(i + 1) * TT].rearrange("(c p) t -> p c t", p=128), V(x1t, 0, [(TT, 8), (1, TT)]), reads=[x1t_t])
                else:
                    for st in range(4):
                        for m in range(8):
                            pb = 6 + m // 4
                            P.op("pe", lambda e, pb=pb, m=m, st=st: e.transpose(
                                ps[pb][:, (m % 4) * 128:(m % 4) * 128 + 128], x1t[:, m * TT + st * 128: m * TT + st * 128 + 128], ident),
                                reads=[x1t_t, ident_t], writes=[pst[pb]], inc=(m % 4 == 3))
                        P.op("act", lambda e: e.copy(out=yo[:, 0:512], in_=ps[6]), reads=[pst[6]], writes=[yo_t])
                        P.op("dve", lambda e: e.tensor_copy(out=yo[:, 512:1024], in_=ps[7]), reads=[pst[7]], writes=[yo_t])
                        P.dma("sp", y_d[i * TT + st * 128: i * TT + st * 128 + 128, :], yo, reads=[yo_t])

        if stop != "pre":
            phase_M()
        P.barrier()
        if stop not in ("M", "pre", "M1", "M2", "M3"):
            for l in range(nl):
                last = (l == nl - 1)
                phase_A(l)
                P.barrier()
                if last and stop in ("A", "A0"):
                    break
                phase_P(l)
                P.barrier()
                if last and stop == "P":
                    break
                phase_Q(l)
                P.barrier()
                if last and stop == "Q":
                    break
                phase_L(l)
                P.barrier()
                if last and stop == "L":
                    write_fin()
                    break
                phase_T(l)
                P.barrier()
                if last and stop == "T":
                    break
                phase_C1(l)
                P.barrier()
                if last and stop == "C1":
                    break
                phase_C2(l, last and stop != "C2")
                P.barrier()
                if last:
                    write_fin()
        P.barrier()
        P.emit(lambda name: es.enter_context(nc.semaphore(name)))
    return nc


def host_inputs(inp, consts=None):
    if consts is None:
        consts = host_consts()
    f = np.float32
    shared = {k: np.ascontiguousarray(inp[k], dtype=f) for k in ("w_mod", "w_in", "w_branch", "w_out", "w_up", "w_down", "w_pool")}
    lbd = np.zeros((NL, 2, 2, 4, 128, 128), f)
    for ai, name in enumerate(("lru_w_a", "lru_w_i")):
        w = inp[name]
        for c in range(4):
            for hb in range(2):
                lbd[:, :, ai, c, hb * 64:(hb + 1) * 64, hb * 64:(hb + 1) * 64] = w[:, :, 2 * c + hb]
    shared["lru_bd"] = lbd
    rbp = np.zeros((NL, RB_LEN), f)
    rbp[:, RB_PAD:RB_PAD + 8 * 15 * 31] = inp["rel_bias"][:, :, ::-1, :].reshape(NL, -1)
    shared["rbp"] = rbp
    for k in ("ivt", "ident", "bones", "permf", "ropeC", "ropeS", "mint", "medge", "pinv"):
        shared[k] = consts[k]
    perm = np.arange(128)
    perm = (perm // 16 ^ 1) * 16 + perm % 16
    maps = []
    for b in range(8):
        m = dict(shared)
        m["x"] = np.ascontiguousarray(np.concatenate([inp["x_sample"][b], inp["x_prompt"][2 * b], inp["x_prompt"][2 * b + 1]], axis=0), dtype=f)
        m["ck"] = np.ascontiguousarray(inp["cache_k"][b].reshape(NL, 512, 512), dtype=f)
        m["cv"] = np.ascontiguousarray(inp["cache_v"][b].reshape(NL, 512, 512), dtype=f)
        pvec = np.zeros((128, NPV), f)

        def put(name, l, arr):
            o, w = PV[(name, l)]
            pvec[:, o:o + w] = arr

        for l in range(NL):
            put("bmod", l, inp["b_mod"][l].reshape(48, 128).T)
            put("g1", l, inp["norm1_g"][l].reshape(8, 128).T)
            put("g2", l, inp["norm2_g"][l].reshape(8, 128).T)
            qg = np.tile(inp["q_norm_g"][l], 2)
            kg = np.tile(inp["k_norm_g"][l], 2)
            put("qg", l, qg[:, None])
            put("qgp", l, qg[perm][:, None])
            put("kg", l, kg[:, None])
            put("kgp", l, kg[perm][:, None])
            put("pscale", l, inp["pool_scale"][l].reshape(4, 128).T)
            put("convw", l, inp["conv_w"][l].reshape(4, 4, 128).transpose(2, 0, 1).reshape(128, 16))
            put("convb", l, inp["conv_b"][l].reshape(4, 128).T)
            put("ba", l, inp["lru_b_a"][l].reshape(2, 4, 128).transpose(2, 0, 1).reshape(128, 8))
            put("bi", l, inp["lru_b_i"][l].reshape(2, 4, 128).transpose(2, 0, 1).reshape(128, 8))
            put("lam", l, inp["lru_lambda"][l].reshape(2, 4, 128).transpose(2, 0, 1).reshape(128, 8))
            put("h0", l, inp["state_h"][b, l].reshape(2, 4, 128).transpose(2, 0, 1).reshape(128, 8))
        o, _ = PV["cond"]
        cs = inp["c"][b].reshape(8, 128).T
        cc = inp["c_ctx"].reshape(8, 128).T
        pvec[:, o:o + 16:2] = cs
        pvec[:, o + 1:o + 16:2] = cc
        m["pvec"] = pvec
        maps.append(m)
    return maps


_NC_CACHE = {}


def kernel(**inputs):
    inp = {k: np.asarray(v) for k, v in inputs.items()}
    if "nc" not in _NC_CACHE:
        _NC_CACHE["nc"] = build()
    nc = _NC_CACHE["nc"]
    maps = host_inputs(inp)
    res = run_bass_kernel_spmd(nc, maps, core_ids=list(range(8)))
    y_prompt = np.zeros((16, 256, D), np.float32)
    y_sample = np.zeros((8, NS, D), np.float32)
    nk = np.zeros((16, NL, 256, 8, 64), np.float32)
    nv = np.zeros((16, NL, 256, 8, 64), np.float32)
    nh = np.zeros((16, NL, 2, 512), np.float32)
    for b in range(8):
        r = res.results[b]
        y = r["y"]
        y_sample[b] = y[:NS]
        y_prompt[2 * b] = y[NS:NS + 256]
        y_prompt[2 * b + 1] = y[NS + 256:]
        nk[2 * b:2 * b + 2] = r["nk"].reshape(2, NL, 256, 8, 64)
        nv[2 * b:2 * b + 2] = r["nv"].reshape(2, NL, 256, 8, 64)
        nh[2 * b:2 * b + 2] = r["nh"].reshape(2, NL, 2, 512)
    return (y_prompt, y_sample, nk, nv, nh)
```

```python
import numpy as np
from contextlib import ExitStack
import concourse.bass as bass
import concourse.mybir as mybir
from concourse.bass_utils import run_bass_kernel_spmd

F32 = mybir.dt.float32
BF16 = mybir.dt.bfloat16
AF = mybir.ActivationFunctionType
ALU = mybir.AluOpType

EPOCH = 8192
NSLOT = 24


class T:
    __slots__ = ("lw", "rd", "name")

    def __init__(self, name=""):
        self.lw = None
        self.rd = {}
        self.name = name


class Prog:
    ENGS = ("pe", "act", "dve", "pool", "sp")

    def __init__(self, nc):
        self.nc = nc
        self.stream = {e: [] for e in self.ENGS}
        self.count = {e: 0 for e in self.ENGS}
        self.known = {e: {} for e in self.ENGS}
        self.slot_val = [0] * (3 * NSLOT)
        self.slot_next = {"sp": 0, "pool": 0, "act": 0}

    def _eng_event(self, eng, idx):
        return ((eng, (idx - 1) // EPOCH), (idx - 1) % EPOCH + 1)

    def _need(self, eng, ev, waits):
        if ev is None:
            return
        key, val = ev
        if self.known[eng].get(key, 0) >= val:
            return
        if waits.get(key, 0) < val:
            waits[key] = val

    def _collect(self, eng, reads, writes):
        waits = {}
        for t in reads:
            self._need(eng, t.lw, waits)
        for t in writes:
            self._need(eng, t.lw, waits)
            for k, v in t.rd.items():
                self._need(eng, (k, v), waits)
        return waits

    def _emit_waits(self, eng, waits):
        for key, val in waits.items():
            if key[0] == eng and eng == "pe":
                continue
            self.stream[eng].append(("wait", key, val))
            self.known[eng][key] = val
            if key[0] in self.ENGS:
                for ep in range(key[1]):
                    self.known[eng][(key[0], ep)] = EPOCH

    def _record(self, ev, reads, writes):
        k, v = ev
        for t in reads:
            if t.rd.get(k, 0) < v:
                t.rd[k] = v
        for t in writes:
            t.lw = ev
            t.rd = {}

    @staticmethod
    def _flat(ts):
        out = []
        for t in ts:
            if isinstance(t, (list, tuple)):
                out.extend(Prog._flat(t))
            else:
                out.append(t)
        return out

    def op(self, eng, fn, reads=(), writes=(), inc=True):
        reads, writes = self._flat(reads), self._flat(writes)
        waits = self._collect(eng, reads, writes)
        self._emit_waits(eng, waits)
        if inc:
            self.count[eng] += 1
            ev = self._eng_event(eng, self.count[eng])
            self.stream[eng].append(("op", fn, ev[0]))
        else:
            ev = self._eng_event(eng, self.count[eng] + 1)
            self.stream[eng].append(("op", fn, None))
        self._record(ev, reads, writes)
        return ev

    def dma(self, q, out_ap, in_ap, reads=(), writes=()):
        reads, writes = self._flat(reads), self._flat(writes)
        base = {"sp": 0, "pool": NSLOT, "act": 2 * NSLOT}[q]
        nsl = NSLOT if q != "act" else 12
        s = base + self.slot_next[q]
        self.slot_next[q] = (self.slot_next[q] + 1) % nsl
        key = ("dma", s)
        waits = self._collect(q, reads, writes)
        if self.slot_val[s] > 0:
            self._need(q, (key, self.slot_val[s]), waits)
        self._emit_waits(q, waits)
        self.slot_val[s] += 16
        ev = (key, self.slot_val[s])
        self.stream[q].append(("dma", out_ap, in_ap, key))
        self._record(ev, reads, writes)
        return ev

    def barrier(self):
        evs = []
        for e in self.ENGS:
            if self.count[e] > 0:
                evs.append(self._eng_event(e, self.count[e]))
        for s, v in enumerate(self.slot_val):
            if v > 0:
                evs.append((("dma", s), v))
        for e in self.ENGS:
            waits = {}
            for ev in evs:
                if ev[0][0] == e:
                    continue
                self._need(e, ev, waits)
            self._emit_waits(e, waits)

    def emit(self, sem_alloc):
        nc = self.nc
        keys = set()
        for e in self.ENGS:
            for it in self.stream[e]:
                if it[0] == "wait":
                    keys.add(it[1])
                elif it[0] == "op" and it[2] is not None:
                    keys.add(it[2])
                elif it[0] == "dma":
                    keys.add(it[3])
        keys = sorted(keys, key=str)
        sems = {k: sem_alloc("s_%s_%s" % (k[0], k[1])) for k in keys}
        engobj = {"pe": "tensor", "act": "scalar", "dve": "vector", "pool": "gpsimd", "sp": "sync"}
        with nc.Block() as block:
            for e in self.ENGS:
                items = self.stream[e]
                if not items:
                    continue

                def body(eobj, items=items):
                    for it in items:
                        if it[0] == "wait":
                            eobj.wait_ge(sems[it[1]], it[2])
                        elif it[0] == "op":
                            ins = it[1](eobj)
                            if it[2] is not None:
                                ins.then_inc(sems[it[2]], 1)
                        else:
                            eobj.dma_start(out=it[1], in_=it[2]).then_inc(sems[it[3]], 16)

                getattr(block, engobj[e])(body)


NT, NS, TT, NTILE = 4608, 4096, 512, 9
D, KC = 1024, 8
NL = 2
EPS = 1e-6
PADN = 4672
P_S0, P_P0, P_PST = 16, 4128, 272
NEG = -240000.0
E_MAX = 10
NY = 22
RB_PAD = 256
RB_LEN = RB_PAD + 8 * 15 * 31 + 1024

PV = {}
_o = 0
for _l in range(NL):
    for _n, _w in (("bmod", 48), ("g1", 8), ("g2", 8), ("qg", 1), ("qgp", 1), ("kg", 1), ("kgp", 1), ("pscale", 4),
                   ("convw", 16), ("convb", 4), ("ba", 8), ("bi", 8), ("lam", 8), ("h0", 8)):
        PV[(_n, _l)] = (_o, _w)
        _o += _w
PV["cond"] = (_o, 16)
_o += 16
NPV = _o

P_BND = [P_S0, P_S0 + NS - 8, P_P0, P_P0 + 248, P_P0 + P_PST, P_P0 + P_PST + 248]
TOP_KR = [0, 2, 4, 6, 8, 10]
BOT_KR = [52, 54, 56, 58, 60, 62]


def _r0(rq):
    return int(np.clip(rq - 4, 0, 56))


def host_consts():
    c = {}
    c["ident"] = np.eye(128, dtype=np.float32)
    bo = np.zeros((128, 128), np.float32)
    bo[:64, :64] = 1.0
    bo[64:, 64:] = 1.0
    c["bones"] = bo
    pm = np.arange(128)
    pm = (pm // 16 ^ 1) * 16 + pm % 16
    pf = np.zeros((128, 128), np.float32)
    pf[pm, np.arange(128)] = 1.0
    c["permf"] = pf
    t = np.arange(NS)
    nf = 16
    inv = 10000.0 ** (-np.arange(nf) / nf)
    C = np.zeros((128, NS), np.float32)
    S = np.zeros((128, NS), np.float32)
    for p in range(128):
        d = p % 64
        pos = (t // 64) if d < 32 else (t % 64)
        dd = d % 32
        i = dd % 16
        ang = pos * inv[i]
        C[p] = np.cos(ang).astype(np.float32)
        sn = np.sin(ang).astype(np.float32)
        S[p] = -sn if dd < 16 else sn
    c["ropeC"], c["ropeS"] = C, S
    kcol = np.arange(64)[:, None]
    qcol = np.arange(64)[None, :]
    c0 = np.clip(qcol - 8, 0, 48)
    colv = (kcol >= c0) & (kcol < c0 + 16)
    mint = np.full((128, NY, 64), NEG, np.float32)
    for half in range(2):
        for y in range(NY):
            e = E_MAX - y
            if -4 <= e + half <= 3:
                mint[half * 64:(half + 1) * 64, y, :] = np.where(colv, 0.0, NEG)
    c["mint"] = mint.reshape(128, NY * 64)
    medge = np.full((12, 128, 8, 64), NEG, np.float32)
    for blk, (rq0, krs) in enumerate(((0, TOP_KR), (56, BOT_KR))):
        for j, kr in enumerate(krs):
            for half in range(2):
                for rl in range(8):
                    rq = rq0 + rl
                    k = kr + half
                    if _r0(rq) <= k <= _r0(rq) + 7:
                        medge[blk * 6 + j, half * 64:(half + 1) * 64, rl, :] = np.where(colv, 0.0, NEG)
    c["medge"] = medge.reshape(12, 128, 512)
    pinv = np.zeros((4, PADN), np.float32)
    for gi, w in enumerate((2, 4, 8, 16)):
        for (start, L) in ((P_S0, NS), (P_P0, 256), (P_P0 + P_PST, 256)):
            tt = np.arange(L)
            lo = np.clip(tt - w // 2, 0, L)
            hi = np.clip(tt - w // 2 + w, 0, L)
            pinv[gi, start:start + L] = 1.0 / (hi - lo).astype(np.float32)
    c["pinv"] = pinv
    ivt = np.zeros((4, 6, 8), np.float32)
    for gi in range(4):
        for bi, pos in enumerate(P_BND):
            ivt[gi, bi] = pinv[gi, pos:pos + 8]
    c["ivt"] = np.ascontiguousarray(np.broadcast_to(ivt.reshape(1, 192), (128, 192)))
    return c


def V(ap, off, dims, p0=0, pn=None):
    (ps_, pcount) = ap.ap[0]
    if pn is None:
        pn = pcount - p0
    return bass.AP(ap.tensor, ap.offset + p0 * ps_ + off, [[ps_, pn]] + [[s, c] for (s, c) in dims])


class Arena:
    def __init__(self, tens, ncols):
        self.t = tens
        self.n = ncols
        self.top = 0

    def alloc(self, n, dt=F32):
        words = n if dt == F32 else (n + 1) // 2
        assert self.top + words <= self.n, "arena overflow %d + %d > %d" % (self.top, words, self.n)
        ap = self.t[:, self.top:self.top + words]
        self.top += words
        if dt != F32:
            ap = ap.bitcast(dt)[:, 0:n]
        return ap


def build(nl=NL, stop=None, debug=False):
    nc = bass.Bass("TRN2", target_bir_lowering=False)
    kin = "ExternalInput"
    kout = "ExternalOutput"
    dbg = kout if debug else "Internal"

    def din(name, shape, dt=F32):
        return nc.dram_tensor(name, list(shape), dt, kind=kin).ap()

    def dscr(name, shape, dt):
        if debug:
            return nc.dram_tensor(name, list(shape), dt, kind=kout).ap()
        return nc.dram_tensor(name, list(shape), dt).ap()

    x_d = din("x", [NT, D])
    ck_d = din("ck", [NL, 512, 512])
    cv_d = din("cv", [NL, 512, 512])
    pv_d = din("pvec", [128, NPV])
    wmod_d = din("w_mod", [NL, D, 6 * D])
    win_d = din("w_in", [NL, D, 6144])
    wbr_d = din("w_branch", [NL, 3, 512, D])
    wout_d = din("w_out", [NL, D, D])
    wup_d = din("w_up", [NL, D, 4 * D])
    wdn_d = din("w_down", [NL, 4 * D, D])
    wpool_d = din("w_pool", [NL, 4, 128, 128])
    lbd_d = din("lru_bd", [NL, 2, 2, 4, 128, 128])
    ident_d = din("ident", [128, 128])
    bones_d = din("bones", [128, 128])
    permf_d = din("permf", [128, 128])
    ropeC_d = din("ropeC", [128, NS])
    ropeS_d = din("ropeS", [128, NS])
    rbp_d = din("rbp", [NL, RB_LEN])
    mint_d = din("mint", [128, NY * 64])
    medge_d = din("medge", [12, 128, 512])
    pinv_d = din("pinv", [4, PADN])
    ivt_d = din("ivt", [128, 192])

    y_d = nc.dram_tensor("y", [NT, D], F32, kind=kout).ap()
    nk_d = nc.dram_tensor("nk", [2, NL, 256, 512], F32, kind=kout).ap()
    nv_d = nc.dram_tensor("nv", [2, NL, 256, 512], F32, kind=kout).ap()
    nh_d = nc.dram_tensor("nh", [32, 128], F32, kind=kout).ap()

    XT = dscr("XT", [D, NT], F32)
    ZP = dscr("ZP", [512, NT], F32)
    QR = dscr("QR", [512, NT], F32)
    KR = dscr("KR", [512, NT], F32)
    UL = dscr("UL", [512, NT], F32)
    UG = dscr("UG", [512, NT], BF16)
    GS = dscr("GS", [3072, NT], BF16)
    VS = dscr("VS", [NT, 520], BF16)
    QT = dscr("QT", [512, NT], BF16)
    KT = dscr("KT", [512, NT], BF16)
    PO = dscr("PO", [512, NT], BF16)
    AO = dscr("AO", [512, NT], BF16)
    LO = dscr("LO", [512, NT], BF16)
    H2 = dscr("H2", [D, NT], BF16)

    es = ExitStack()
    with es:
        PERS = 1536
        ARW = 51600
        pers_t = es.enter_context(nc.sbuf_tensor("pers", [128, PERS], F32))
        arena_t = es.enter_context(nc.sbuf_tensor("arena", [128, ARW], F32))
        psb = [es.enter_context(nc.psum_tensor("ps%d" % i, [128, 512], F32)) for i in range(8)]
        ps = [p[:, :] for p in psb]
        pst = [T("ps%d" % i) for i in range(8)]
        P = Prog(nc)
        PA = Arena(pers_t, PERS)
        A = Arena(arena_t, ARW)

        pv = PA.alloc(NPV)
        pv_t = T("pv")
        ident = PA.alloc(128)
        ident_t = T("ident")
        ones_b = PA.alloc(128, BF16)
        ident_b = PA.alloc(128, BF16)
        ones_f = PA.alloc(64)
        bones_b = PA.alloc(128, BF16)
        cst_t = T("cst")
        modv = [PA.alloc(96) for _ in range(NL)]
        A1 = [PA.alloc(16) for _ in range(NL)]
        A2 = [PA.alloc(16) for _ in range(NL)]
        cl8 = [PA.alloc(8) for _ in range(NL)]
        cltmp = [PA.alloc(16) for _ in range(NL)]
        fin = PA.alloc(32)
        fin_t = T("fin")
        mod_t = [T("mod%d" % l) for l in range(NL)]

        def pvc(name, l, i=0, n=1):
            o, w = PV[(name, l)]
            return pv[:, o + i:o + i + n]

        P.dma("sp", pv, pv_d, writes=[pv_t])
        P.dma("sp", ident, ident_d, writes=[ident_t])
        P.dma("pool", bones_b, bones_d, writes=[cst_t])
        P.op("dve", lambda e: e.memset(ones_b, 1.0), writes=[cst_t])
        P.op("dve", lambda e: e.tensor_copy(out=ident_b, in_=ident), reads=[ident_t], writes=[cst_t])
        P.op("dve", lambda e: e.memset(ones_f, 1.0), writes=[cst_t])
        P.op("dve", lambda e: e.memset(fin, 0.0), writes=[fin_t])
        epsc = PA.alloc(1)
        onec = PA.alloc(1)
        P.op("dve", lambda e: e.memset(epsc, EPS), writes=[cst_t])
        P.op("dve", lambda e: e.memset(onec, 1.0), writes=[cst_t])

        def modcol(l, j, grp, c):
            return modv[l][:, j * 48 + grp * 8 + c: j * 48 + grp * 8 + c + 1]

        sc = PA.alloc(16, BF16)
        sc_t = T("sc")
        MBANK = 6
        mstate = {"n": 0}

        def m_setup():
            co, _ = PV["cond"]
            P.op("act", lambda e: e.activation(out=sc, in_=pv[:, co:co + 16], func=AF.Silu), reads=[pv_t], writes=[sc_t])

        def m_slabs(l, s_list, slabs, slab_t):
            for s_ in s_list:
                b = mstate["n"] % 2
                mstate["n"] += 1
                P.dma("pool", V(slabs[b], 0, [(512, 8), (1, 512)]),
                      wmod_d[l, :, s_ * 512:(s_ + 1) * 512].rearrange("(kc p) n -> p kc n", p=128), writes=[slab_t[b]])
                for j in range(4):
                    col = s_ * 4 + j
                    for kc in range(8):
                        P.op("pe", lambda e, b=b, j=j, kc=kc, col=col, slabs=slabs: e.matmul(
                            ps[MBANK][:, 2 * col:2 * col + 2], lhsT=slabs[b][:, kc * 512 + j * 128: kc * 512 + j * 128 + 128],
                            rhs=sc[:, 2 * kc:2 * kc + 2], start=(kc == 0), stop=(kc == 7)),
                            reads=[slab_t[b], sc_t], writes=[pst[MBANK]], inc=(kc == 7))

        def m_finish(l):
            o, _ = PV[("bmod", l)]
            for j in range(2):
                P.op("dve", lambda e, l=l, j=j, o=o: e.tensor_tensor(
                    out=modv[l][:, j * 48:(j + 1) * 48], in0=V(ps[MBANK], j, [(2, 48)]), in1=pv[:, o:o + 48], op=ALU.add),
                    reads=[pst[MBANK], pv_t], writes=[mod_t[l]])
            for j in range(2):
                for (dst, grp, gname) in ((A1, 1, "g1"), (A2, 4, "g2")):
                    go, _ = PV[(gname, l)]
                    P.op("dve", lambda e, l=l, j=j, dst=dst, grp=grp, go=go: e.scalar_tensor_tensor(
                        out=dst[l][:, j * 8:(j + 1) * 8], in0=modv[l][:, j * 48 + grp * 8: j * 48 + grp * 8 + 8],
                        scalar=1.0, in1=pv[:, go:go + 8], op0=ALU.add, op1=ALU.mult),
                        reads=[mod_t[l], pv_t], writes=[mod_t[l]])
            lo_, _ = PV[("lam", l)]
            P.op("act", lambda e, l=l, lo_=lo_: e.activation(out=cltmp[l][:, 0:8], in_=pv[:, lo_:lo_ + 8], func=AF.Sigmoid),
                 reads=[pv_t], writes=[mod_t[l]])
            P.op("act", lambda e, l=l: e.activation(out=cltmp[l][:, 8:16], in_=cltmp[l][:, 0:8], func=AF.Ln),
                 reads=[mod_t[l]], writes=[mod_t[l]])
            P.op("dve", lambda e, l=l: e.tensor_scalar(out=cl8[l], in0=cltmp[l][:, 8:16], scalar1=8.0, scalar2=None, op0=ALU.mult),
                 reads=[mod_t[l]], writes=[mod_t[l]])

        def phase_M():
            A.top = 0
            m_setup()
            slabs = [A.alloc(8 * 512, BF16) for _ in range(2)]
            slab_t = [T("mslab0"), T("mslab1")]
            m_slabs(0, range(12), slabs, slab_t)
            m_finish(0)
            if nl > 1 and stop in ("M", "A", "A0"):
                m_slabs(1, range(12), slabs, slab_t)
                m_finish(1)

        def norm_tile(xT, xT_t, sq, sq_t, rs, rs_t, psn, Amod, Bcol, l, j, out_fn, out_reads_writes):
            raise NotImplementedError

        def phase_A(l):
            A.top = 0
            hT = A.alloc(8 * NT, BF16)
            hT_t = [T("hT%d" % i) for i in range(NTILE)]
            mark = A.top
            xin = [A.alloc(4 * D) for _ in range(2)] if l == 0 else None
            xin_t = [T("xin0"), T("xin1")]
            xT = [A.alloc(8 * TT) for _ in range(3)]
            xT_t = [T("xT0"), T("xT1"), T("xT2")]
            sq = [A.alloc(8 * TT, BF16) for _ in range(2)]
            sq_t = [T("sq0"), T("sq1")]
            rs = [A.alloc(TT) for _ in range(2)]
            rs_t = [T("rs0"), T("rs1")]
            def a0_stage1(i):
                b = i % 2
                j = 0 if i < 8 else 1
                xTv, xTv_t = xT[i % 3], xT_t[i % 3]
                if l == 0:
                    P.dma("sp", V(xin[b], 0, [(D, 4), (1, D)]),
                          x_d[i * TT:(i + 1) * TT, :].rearrange("(st p) f -> p st f", p=128), writes=[xin_t[b]])
                    for half in range(2):
                        for cc in range(4):
                            c = half * 4 + cc
                            for st in range(4):
                                P.op("pe", lambda e, b=b, c=c, cc=cc, st=st: e.transpose(
                                    ps[cc][:, st * 128:(st + 1) * 128], xin[b][:, st * D + c * 128: st * D + c * 128 + 128], ident),
                                    reads=[xin_t[b], ident_t], writes=[pst[cc]], inc=(st == 3))
                            eng = "dve"
                            if eng == "act":
                                P.op("act", lambda e, b=b, c=c, cc=cc: e.copy(out=xTv[:, c * TT:(c + 1) * TT], in_=ps[cc]),
                                     reads=[pst[cc]], writes=[xTv_t])
                            else:
                                P.op("dve", lambda e, b=b, c=c, cc=cc: e.tensor_copy(out=xTv[:, c * TT:(c + 1) * TT], in_=ps[cc]),
                                     reads=[pst[cc]], writes=[xTv_t])
                    P.dma("pool", XT[:, i * TT:(i + 1) * TT].rearrange("(c p) t -> p c t", p=128),
                          V(xTv, 0, [(TT, 8), (1, TT)]), reads=[xTv_t])
                else:
                    P.dma("sp", V(xTv, 0, [(TT, 8), (1, TT)]),
                          XT[:, i * TT:(i + 1) * TT].rearrange("(c p) t -> p c t", p=128), writes=[xTv_t])
                pn = 4 + b
                P.op("act", lambda e, b=b: e.activation(out=sq[b], in_=xTv, func=AF.Square), reads=[xTv_t], writes=[sq_t[b]])
                for c in range(8):
                    P.op("pe", lambda e, b=b, c=c, pn=pn: e.matmul(ps[pn], lhsT=ones_b, rhs=sq[b][:, c * TT:(c + 1) * TT],
                                                                  start=(c == 0), stop=(c == 7)),
                         reads=[sq_t[b], cst_t], writes=[pst[pn]], inc=(c == 7))

            def a0_stage2(i):
                b = i % 2
                j = 0 if i < 8 else 1
                pn = 4 + b
                xTv, xTv_t = xT[i % 3], xT_t[i % 3]
                P.op("act", lambda e, b=b, pn=pn: e.activation(out=rs[b], in_=ps[pn], func=AF.Ln, scale=1.0 / D, bias=epsc),
                     reads=[pst[pn], cst_t], writes=[rs_t[b]])
                P.op("act", lambda e, b=b: e.activation(out=rs[b], in_=rs[b], func=AF.Exp, scale=-0.5), reads=[rs_t[b]], writes=[rs_t[b]])
                P.op("dve", lambda e, b=b: e.tensor_tensor(out=V(xTv, 0, [(TT, 8), (1, TT)]), in0=V(xTv, 0, [(TT, 8), (1, TT)]),
                                                           in1=V(rs[b], 0, [(0, 8), (1, TT)]), op=ALU.mult),
                     reads=[rs_t[b], xTv_t], writes=[xTv_t])

            def a0_stage3(i):
                b = i % 2
                j = 0 if i < 8 else 1
                xTv, xTv_t = xT[i % 3], xT_t[i % 3]
                for c in range(8):
                    if c % 3 == 2:
                        P.op("dve", lambda e, b=b, c=c, i=i, j=j: e.tensor_scalar(
                            out=hT[:, c * NT + i * TT: c * NT + (i + 1) * TT], in0=xTv[:, c * TT:(c + 1) * TT],
                            scalar1=A1[l][:, j * 8 + c: j * 8 + c + 1], scalar2=modcol(l, j, 0, c), op0=ALU.mult, op1=ALU.add),
                            reads=[xTv_t, mod_t[l]], writes=[hT_t[i]])
                        continue
                    P.op("act", lambda e, b=b, c=c, i=i, j=j: e.activation(
                        out=hT[:, c * NT + i * TT: c * NT + (i + 1) * TT], in_=xTv[:, c * TT:(c + 1) * TT], func=AF.Identity,
                        scale=A1[l][:, j * 8 + c: j * 8 + c + 1], bias=modcol(l, j, 0, c)),
                        reads=[xTv_t, mod_t[l]], writes=[hT_t[i]])

            for i in range(NTILE + 2):
                if i < NTILE:
                    a0_stage1(i)
                if 0 <= i - 1 < NTILE:
                    a0_stage2(i - 1)
                if 0 <= i - 2 < NTILE:
                    a0_stage3(i - 2)
            if stop == "A0":
                return hT
            P.barrier()
            A.top = mark
            slabs = [A.alloc(8 * 512, BF16) for _ in range(2)]
            slab_t = [T("slab0"), T("slab1")]
            stg = [A.alloc(NT) for _ in range(2)]
            stg_t = [T("stg0"), T("stg1")]
            vst = [A.alloc(520, BF16) for _ in range(2)]
            vst_t = [T("vst0"), T("vst1")]
            for vb_ in range(2):
                P.op("pool", lambda e, vb_=vb_: e.memset(vst[vb_], 1.0), writes=[vst_t[vb_]])
            vsf = [A.alloc(512) for _ in range(2)]
            vsf_t = [T("vsf0"), T("vsf1")]
            gtmp = [(A.alloc(TT), A.alloc(TT)) for _ in range(2)]
            gtmp_t = [T("gt0"), T("gt1")]
            dests = {0: (ZP, F32, None), 1: (QR, F32, None), 2: (KR, F32, None), 4: (UL, F32, None), 5: (UG, BF16, "gelu")}
            nsub = 0
            npsum = 0
            order = [3, 1, 2, 0, 4, 5, 6, 7, 8, 9, 10, 11]
            import os as _os
            if _os.environ.get('KORDER'):
                order = [int(v) for v in _os.environ['KORDER'].split(',')]
            for si, s in enumerate(order):
                sb_ = si % 2
                P.dma("pool", V(slabs[sb_], 0, [(512, 8), (1, 512)]),
                      win_d[l, :, s * 512:(s + 1) * 512].rearrange("(kc p) n -> p kc n", p=128), writes=[slab_t[sb_]])
                if s == 3:
                    for tt in range(NT // 128):
                        pb = npsum % 8
                        npsum += 1
                        vb = tt % 2
                        for kc in range(8):
                            P.op("pe", lambda e, pb=pb, kc=kc, tt=tt, sb_=sb_: e.matmul(
                                ps[pb], lhsT=hT[:, kc * NT + tt * 128: kc * NT + tt * 128 + 128],
                                rhs=slabs[sb_][:, kc * 512:(kc + 1) * 512], start=(kc == 0), stop=(kc == 7)),
                                reads=[hT_t[tt // 4], slab_t[sb_]], writes=[pst[pb]], inc=(kc == 7))
                        P.op("act", lambda e, pb=pb, vb=vb: e.copy(out=V(vst[vb], 0, [(65, 8), (1, 64)]), in_=V(ps[pb], 0, [(64, 8), (1, 64)])),
                             reads=[pst[pb]], writes=[vst_t[vb]])
                        if _os.environ.get('KV') != '2':
                            P.dma("sp", VS[tt * 128:(tt + 1) * 128, :], vst[vb], reads=[vst_t[vb]])
                        if tt >= 32 and _os.environ.get('KV') != '1':
                            P.op("dve", lambda e, pb=pb, vb=vb: e.tensor_copy(out=vsf[vb], in_=ps[pb]), writes=[pst[pb], vsf_t[vb]])
                            sq_ = (tt - 32) // 2
                            hf = (tt - 32) % 2
                            P.dma("sp", nv_d[sq_, l, hf * 128:(hf + 1) * 128, :], vsf[vb], reads=[vsf_t[vb]])
                    continue
                for jj in range(4):
                    gb = nsub % 2
                    nsub += 1
                    if s in dests:
                        dst, dt, fn = dests[s]
                        drow = jj * 128
                    else:
                        dst, dt, fn = GS, BF16, "sig"
                        drow = (s - 6) * 512 + jj * 128
                    sview = stg[gb] if dt == F32 else stg[gb].bitcast(BF16)[:, 0:NT]
                    for i in range(NTILE):
                        pb = npsum % 8
                        npsum += 1
                        for kc in range(8):
                            P.op("pe", lambda e, pb=pb, kc=kc, i=i, sb_=sb_, jj=jj: e.matmul(
                                ps[pb], lhsT=slabs[sb_][:, kc * 512 + jj * 128: kc * 512 + jj * 128 + 128],
                                rhs=hT[:, kc * NT + i * TT: kc * NT + (i + 1) * TT], start=(kc == 0), stop=(kc == 7)),
                                reads=[hT_t[i], slab_t[sb_]], writes=[pst[pb]], inc=(kc == 7))
                        o_ap = sview[:, i * TT:(i + 1) * TT]
                        if fn == "gelu":
                            gelu_from_psum(P, ps[pb], pst[pb], o_ap, stg_t[gb], gtmp, gtmp_t, npsum)
                        elif fn == "sig":
                            P.op("act", lambda e, pb=pb, o_ap=o_ap: e.activation(out=o_ap, in_=ps[pb], func=AF.Sigmoid),
                                 reads=[pst[pb]], writes=[stg_t[gb]])
                        elif i % 2 == 0:
                            P.op("act", lambda e, pb=pb, o_ap=o_ap: e.copy(out=o_ap, in_=ps[pb]), reads=[pst[pb]], writes=[stg_t[gb]])
                        else:
                            P.op("dve", lambda e, pb=pb, o_ap=o_ap: e.tensor_copy(out=o_ap, in_=ps[pb]), reads=[pst[pb]], writes=[stg_t[gb]])
                    P.dma("sp", dst[drow:drow + 128, :], sview, reads=[stg_t[gb]])
            return hT

        gtmp = None
        gtmp_t = None

        def gelu_from_psum(P, psrc, psrc_t, o_ap, o_t, gtmp, gtmp_t, n):
            b = n % 2
            g0, g1 = gtmp[b]
            t = gtmp_t[b]
            P.op("act", lambda e: e.activation(out=g0, in_=psrc, func=AF.Square), reads=[psrc_t], writes=[t])
            P.op("dve", lambda e: e.tensor_scalar(out=g0, in0=g0, scalar1=0.044715, scalar2=1.0, op0=ALU.mult, op1=ALU.add),
                 reads=[t], writes=[t])
            P.op("dve", lambda e: e.tensor_tensor(out=g0, in0=g0, in1=psrc, op=ALU.mult), reads=[t, psrc_t], writes=[t])
            P.op("act", lambda e: e.activation(out=g1, in_=g0, func=AF.Sigmoid, scale=1.5957691216), reads=[t], writes=[t])
            P.op("dve", lambda e: e.tensor_tensor(out=o_ap, in0=g1, in1=psrc, op=ALU.mult), reads=[t, psrc_t], writes=[o_t])

        def ptile(buf, i, p0=0, pn=None):
            if i < 8:
                return V(buf, P_S0 + i * TT, [(1, TT)], p0=p0, pn=pn)
            return V(buf, P_P0, [(P_PST, 2), (1, 256)], p0=p0, pn=pn)

        def as3(ap512):
            return V(ap512, 0, [(256, 2), (1, 256)])

        def load_padded(q, buf, buf_t, src_rows, dt_note=None):
            P.dma(q, V(buf, P_S0, [(1, NS)]), src_rows[:, 0:NS], writes=[buf_t])
            P.dma(q, V(buf, P_P0, [(P_PST, 2), (1, 256)]), src_rows[:, NS:NT].rearrange("p (s t) -> p s t", s=2), writes=[buf_t])

        def phase_P(l):
            A.top = 0
            Ub = [A.alloc(PADN) for _ in range(2)]
            S1b = [A.alloc(PADN) for _ in range(2)]
            S2b = [A.alloc(PADN) for _ in range(2)]
            IVb = [None, None]
            Dbb = [A.alloc(PADN, BF16) for _ in range(2)]
            wpb = [A.alloc(128, BF16) for _ in range(2)]
            stg = [A.alloc(NT, BF16) for _ in range(2)]
            Ut = [T("U0"), T("U1")]
            S1t = [T("S10"), T("S11")]
            S2t = [T("S20"), T("S21")]
            IVt = [T("IV0"), T("IV1")]
            Dbt = [T("Db0"), T("Db1")]
            wpt = [T("wp0"), T("wp1")]
            stg_t = [T("pstg0"), T("pstg1")]
            for k in range(2):
                P.op("pool", lambda e, k=k: e.memset(Ub[k], 0.0), writes=[Ut[k]])
            mslabs = [A.alloc(8 * 512, BF16) for _ in range(2)]
            mslab_t = [T("mslabP0"), T("mslabP1")]
            ivt = A.alloc(192)
            ivt_t = T("ivt")
            P.dma("sp", ivt, ivt_d, writes=[ivt_t])
            tmp8 = A.alloc(8)
            tmp8_t = T("tmp8")
            npb = 0
            def p_loads(g):
                k = g % 2
                load_padded("sp", Ub[k], Ut[k], ZP[g * 128:(g + 1) * 128, :])
                P.dma("pool", wpb[k], wpool_d[l, g], writes=[wpt[k]])

            p_loads(0)
            for g in range(4):
                k = g % 2
                U, S1, S2, IV, Db, wp = Ub[k], S1b[k], S2b[k], IVb[k], Dbb[k], wpb[k]
                U_t, S1_t, S2_t, IV_t, Db_t, wp_t = Ut[k], S1t[k], S2t[k], IVt[k], Dbt[k], wpt[k]
                if g < 3:
                    p_loads(g + 1)
                cur, cur_t = U, U_t
                bufs = [(S1, S1_t), (S2, S2_t)]
                for lev in range(g + 1):
                    dst, dst_t = bufs[lev % 2]
                    if lev == 0:
                        lo, hi, sa, sb2 = 1, PADN - 1, -1, 0
                    else:
                        sh = 1 << (lev - 1)
                        lo, hi, sa, sb2 = (1 << lev), PADN - (1 << lev), -sh, sh
                    n = hi - lo
                    P.op("dve", lambda e, dst=dst, cur=cur, lo=lo, n=n, sa=sa, sb2=sb2: e.tensor_tensor(
                        out=dst[:, lo:lo + n], in0=cur[:, lo + sa:lo + sa + n], in1=cur[:, lo + sb2:lo + sb2 + n], op=ALU.add),
                        reads=[cur_t], writes=[dst_t])
                    cur, cur_t = dst, dst_t
                lo, n = 16, PADN - 32
                wv = float(1.0 / (2 << g))
                P.op("dve", lambda e, cur=cur, Db=Db, U=U, lo=lo, n=n, wv=wv: e.scalar_tensor_tensor(
                    out=Db[:, lo:lo + n], in0=cur[:, lo:lo + n], scalar=wv, in1=U[:, lo:lo + n], op0=ALU.mult, op1=ALU.subtract),
                    reads=[cur_t, U_t], writes=[Db_t])
                for bi, pos in enumerate(P_BND):
                    P.op("dve", lambda e, cur=cur, pos=pos, g=g, bi=bi: e.tensor_tensor(
                        out=tmp8, in0=cur[:, pos:pos + 8], in1=ivt[:, (g * 6 + bi) * 8:(g * 6 + bi) * 8 + 8], op=ALU.mult),
                        reads=[cur_t, ivt_t], writes=[tmp8_t])
                    P.op("dve", lambda e, Db=Db, U=U, pos=pos: e.tensor_tensor(
                        out=Db[:, pos:pos + 8], in0=tmp8, in1=U[:, pos:pos + 8], op=ALU.subtract),
                        reads=[tmp8_t, U_t], writes=[Db_t])
                sb_ = g % 2
                po, _ = PV[("pscale", l)]
                for i in range(NTILE):
                    pb = npb % 4
                    npb += 1
                    o_ps = ps[pb] if i < 8 else as3(ps[pb])
                    P.op("pe", lambda e, o_ps=o_ps, i=i, wp=wp, Db=Db: e.matmul(o_ps, lhsT=wp, rhs=ptile(Db, i), start=True, stop=True),
                         reads=[wp_t, Db_t], writes=[pst[pb]])
                    P.op("act", lambda e, pb=pb, i=i, sb_=sb_, g=g, po=po: e.activation(
                        out=stg[sb_][:, i * TT:(i + 1) * TT], in_=ps[pb], func=AF.Copy, scale=pv[:, po + g:po + g + 1]),
                        reads=[pst[pb], pv_t], writes=[stg_t[sb_]])
                P.dma("act", PO[g * 128:(g + 1) * 128, :], stg[sb_], reads=[stg_t[sb_]])
                if l == 0 and nl > 1:
                    m_slabs(1, range(3 * g, 3 * g + 3), mslabs, mslab_t)
                    if g == 3:
                        m_finish(1)

        def phase_Q(l):
            A.top = 0
            rC = A.alloc(NS)
            rS = A.alloc(NS)
            rope_t = [T("ropeC"), T("ropeS")]
            P.dma("sp", rC, ropeC_d, writes=[rope_t[0]])
            P.dma("sp", rS, ropeS_d, writes=[rope_t[1]])
            permf = A.alloc(128)
            perm_t = T("permf")
            P.dma("sp", permf, permf_d, writes=[perm_t])
            raw = [A.alloc(NT) for _ in range(2)]
            rawp = [A.alloc(NS) for _ in range(2)]
            raw_t = [T("raw0"), T("raw1")]
            rawp_t = [T("rawp0"), T("rawp1")]
            sqb = [A.alloc(NT, BF16) for _ in range(2)]
            sqb_t = [T("qsq0"), T("qsq1")]
            rstdb = [A.alloc(NT) for _ in range(2)]
            rstdb_t = [T("rstd0"), T("rstd1")]
            outb = [A.alloc(NT, BF16) for _ in range(2)]
            outb_t = [T("outb0"), T("outb1")]
            kf = A.alloc(512)
            kf_t = T("kf")
            kout = A.alloc(512)
            kout_t = T("kout")
            n = 0
            npb = 0
            for which, (src, dst, gname) in enumerate(((QR, QT, "qg"), (KR, KT, "kg"))):
                go, _ = PV[(gname, l)]
                gpo, _ = PV[(gname + "p", l)]
                gcol = pv[:, go:go + 1]
                gpcol = pv[:, gpo:gpo + 1]
                for c in range(4):
                    b = n % 2
                    n += 1
                    sq, sq_t, rstd, rstd_t = sqb[b], sqb_t[b], rstdb[b], rstdb_t[b]
                    v1, v2, v_t = raw[b][:, 0:NS], rawp[b], raw_t[b]
                    if n == 1:
                        P.dma("sp", raw[b], src[c * 128:(c + 1) * 128, :], writes=[raw_t[b]])
                    if n < 8:
                        nsrc = (QR, KR)[n // 4]
                        nc_ = n % 4
                        P.dma("sp", raw[1 - b], nsrc[nc_ * 128:(nc_ + 1) * 128, :], writes=[raw_t[1 - b]])

                    P.op("act", lambda e, b=b, sq=sq: e.activation(out=sq, in_=raw[b], func=AF.Square), reads=[raw_t[b]], writes=[sq_t])
                    for i in range(NTILE):
                        pb = npb % 4
                        npb += 1
                        P.op("pe", lambda e, pb=pb, i=i, sq=sq: e.matmul(ps[pb], lhsT=bones_b, rhs=sq[:, i * TT:(i + 1) * TT], start=True, stop=True),
                             reads=[sq_t, cst_t], writes=[pst[pb]])
                        P.op("act", lambda e, pb=pb, i=i, rstd=rstd: e.activation(out=rstd[:, i * TT:(i + 1) * TT], in_=ps[pb], func=AF.Ln,
                                                                      scale=1.0 / 64, bias=epsc),
                             reads=[pst[pb], cst_t], writes=[rstd_t])
                    P.op("act", lambda e, rstd=rstd: e.activation(out=rstd, in_=rstd, func=AF.Exp, scale=-0.5), reads=[rstd_t], writes=[rstd_t])
                    for i in range(8):
                        pb = npb % 4
                        npb += 1
                        P.op("pe", lambda e, pb=pb, i=i, b=b: e.matmul(ps[pb], lhsT=permf, rhs=raw[b][:, i * TT:(i + 1) * TT], start=True, stop=True),
                             reads=[raw_t[b], perm_t], writes=[pst[pb]])
                        P.op("dve", lambda e, pb=pb, i=i, gpcol=gpcol, v2=v2: e.scalar_tensor_tensor(
                            out=v2[:, i * TT:(i + 1) * TT], in0=ps[pb], scalar=gpcol, in1=rS[:, i * TT:(i + 1) * TT], op0=ALU.mult, op1=ALU.mult),
                            reads=[pst[pb], rope_t, pv_t], writes=[rawp_t[b]])
                    P.op("dve", lambda e, b=b, gcol=gcol, v1=v1: e.scalar_tensor_tensor(out=v1, in0=raw[b][:, 0:NS], scalar=gcol, in1=rC,
                                                                                       op0=ALU.mult, op1=ALU.mult),
                         reads=[rope_t, pv_t], writes=[raw_t[b]])
                    P.op("dve", lambda e, v1=v1, v2=v2: e.tensor_tensor(out=v1, in0=v1, in1=v2, op=ALU.add), reads=[rawp_t[b]], writes=[raw_t[b]])
                    P.op("dve", lambda e, b=b, v1=v1, rstd=rstd: e.tensor_tensor(out=outb[b][:, 0:NS], in0=v1, in1=rstd[:, 0:NS], op=ALU.mult),
                         reads=[raw_t[b], rstd_t], writes=[outb_t[b]])
                    P.op("dve", lambda e, b=b, gcol=gcol, rstd=rstd: e.scalar_tensor_tensor(out=outb[b][:, NS:NT], in0=raw[b][:, NS:NT], scalar=gcol,
                                                                                           in1=rstd[:, NS:NT], op0=ALU.mult, op1=ALU.mult),
                         reads=[raw_t[b], rstd_t, pv_t], writes=[outb_t[b]])
                    P.dma("pool", dst[c * 128:(c + 1) * 128, :], outb[b], reads=[outb_t[b]])
                    if which == 1:
                        P.op("dve", lambda e, b=b, gcol=gcol, rstd=rstd: e.scalar_tensor_tensor(out=kf, in0=raw[b][:, NS:NT], scalar=gcol,
                                                                                               in1=rstd[:, NS:NT], op0=ALU.mult, op1=ALU.mult),
                             reads=[raw_t[b], rstd_t, pv_t], writes=[kf_t])
                        pb = npb % 4
                        npb += 1
                        for tt in range(4):
                            P.op("pe", lambda e, pb=pb, tt=tt: e.transpose(ps[pb][:, tt * 128:(tt + 1) * 128], kf[:, tt * 128:(tt + 1) * 128], ident),
                                 reads=[kf_t, ident_t], writes=[pst[pb]], inc=(tt == 3))
                        P.op("act", lambda e, pb=pb: e.copy(out=kout, in_=ps[pb]), reads=[pst[pb]], writes=[kout_t])
                        for s_ in range(2):
                            P.dma("pool", nk_d[s_, l, :, c * 128:(c + 1) * 128].rearrange("(h p) f -> p h f", p=128),
                                  V(kout, s_ * 256, [(128, 2), (1, 128)]), reads=[kout_t])

        def phase_L(l):
            A.top = 0
            U = A.alloc(PADN)
            XC = A.alloc(PADN)
            XCb = A.alloc(PADN, BF16)
            Rb = [A.alloc(NT), A.alloc(NT)]
            Ib = [A.alloc(NT), A.alloc(NT)]
            TMb = [A.alloc(NT), A.alloc(NT)]
            H = TMb
            UGbb = [A.alloc(NT, BF16) for _ in range(2)]
            ob = A.alloc(NT, BF16)
            wbdb = [[A.alloc(128, BF16) for _ in range(4)] for _ in range(2)]
            U_t, XC_t, XCb_t, ob_t = (T("lU"), T("lXC"), T("lXCb"), T("lob"))
            UGt_ = [T("lUG0"), T("lUG1")]
            wbdt_ = [T("lwbd0"), T("lwbd1")]
            Rt_ = [T("lR0"), T("lR1")]
            It_ = [T("lI0"), T("lI1")]
            TMt_ = [T("lTM0"), T("lTM1")]
            H_t = TMt_
            P.op("dve", lambda e: e.memset(U, 0.0), writes=[U_t])
            cwo, _ = PV[("convw", l)]
            cbo, _ = PV[("convb", l)]
            bao, _ = PV[("ba", l)]
            bio, _ = PV[("bi", l)]
            h0o, _ = PV[("h0", l)]
            npb = 0

            def rev(ap):
                (ps_, pn), (fs, fn) = ap.ap
                return bass.AP(ap.tensor, ap.offset + (fn - 1) * fs, [[ps_, pn], [-fs, fn]])

            def l_loads(c):
                load_padded("sp", U, U_t, UL[c * 128:(c + 1) * 128, :])
                P.dma("sp", UGbb[c % 2], UG[c * 128:(c + 1) * 128, :], writes=[UGt_[c % 2]])
                for d in range(2):
                    for ai in range(2):
                        P.dma("pool", wbdb[c % 2][d * 2 + ai], lbd_d[l, d, ai, c], writes=[wbdt_[c % 2]])

            l_loads(0)
            for c in range(4):
                UGb, UG_t = UGbb[c % 2], UGt_[c % 2]
                wbd, wbd_t = wbdb[c % 2], wbdt_[c % 2]
                lo, n = 1, PADN - 3
                P.op("act", lambda e, c=c, lo=lo, n=n: e.activation(out=XC[:, lo:lo + n], in_=U[:, lo - 1:lo - 1 + n], func=AF.Identity,
                                                                    scale=pv[:, cwo + c:cwo + c + 1], bias=pv[:, cbo + c:cbo + c + 1]),
                     reads=[U_t, pv_t], writes=[XC_t])
                for j in range(1, 4):
                    P.op("dve", lambda e, c=c, j=j, lo=lo, n=n: e.scalar_tensor_tensor(
                        out=XC[:, lo:lo + n], in0=U[:, lo - 1 + j:lo - 1 + j + n], scalar=pv[:, cwo + j * 4 + c:cwo + j * 4 + c + 1],
                        in1=XC[:, lo:lo + n], op0=ALU.mult, op1=ALU.add), reads=[U_t, XC_t, pv_t], writes=[XC_t])
                P.op("dve", lambda e, lo=lo, n=n: e.tensor_copy(out=XCb[:, lo:lo + n], in_=XC[:, lo:lo + n]), reads=[XC_t], writes=[XCb_t])
                if c < 3:
                    l_loads(c + 1)
                for d in range(2):
                    R, I, TM = Rb[d], Ib[d], TMb[d]
                    R_t, I_t, TM_t = Rt_[d], It_[d], TMt_[d]
                    for i in range(NTILE):
                        for ai, (dstb, dst_t, bo) in enumerate(((R, R_t, bao), (I, I_t, bio))):
                            pb = npb % 4
                            npb += 1
                            o_ps = ps[pb] if i < 8 else as3(ps[pb])
                            P.op("pe", lambda e, o_ps=o_ps, i=i, d=d, ai=ai, wbd=wbd: e.matmul(o_ps, lhsT=wbd[d * 2 + ai], rhs=ptile(XCb, i),
                                                                                      start=True, stop=True),
                                 reads=[wbd_t, XCb_t], writes=[pst[pb]])
                            P.op("act", lambda e, pb=pb, i=i, dstb=dstb, bo=bo, d=d, c=c: e.activation(
                                out=dstb[:, i * TT:(i + 1) * TT], in_=ps[pb], func=AF.Sigmoid, bias=pv[:, bo + d * 4 + c:bo + d * 4 + c + 1]),
                                reads=[pst[pb], pv_t], writes=[dst_t])
                    P.op("act", lambda e, d=d, c=c, R=R: e.activation(out=R, in_=R, func=AF.Exp, scale=cl8[l][:, d * 4 + c:d * 4 + c + 1]),
                         reads=[R_t, mod_t[l]], writes=[R_t])
                    P.op("act", lambda e, R=R, TM=TM: e.activation(out=TM, in_=R, func=AF.Square), reads=[R_t], writes=[TM_t])
                    P.op("act", lambda e, TM=TM: e.activation(out=TM, in_=TM, func=AF.Sqrt, scale=-1.0, bias=onec), reads=[TM_t, cst_t], writes=[TM_t])
                    P.op("dve", lambda e, I=I, TM=TM: e.tensor_tensor(out=I, in0=I, in1=TM, op=ALU.mult), reads=[I_t, TM_t], writes=[I_t])
                    P.op("dve", lambda e, I=I: e.tensor_tensor(out=I[:, 0:NS], in0=I[:, 0:NS], in1=XC[:, P_S0:P_S0 + NS], op=ALU.mult),
                         reads=[I_t, XC_t], writes=[I_t])
                    P.op("dve", lambda e, I=I: e.tensor_tensor(out=as3(I[:, NS:NT]), in0=as3(I[:, NS:NT]), in1=ptile(XC, 8), op=ALU.mult),
                         reads=[I_t, XC_t], writes=[I_t])
                    Hd = H[d]
                    segs = [(0, NS, pv[:, h0o + d * 4 + c:h0o + d * 4 + c + 1]), (NS, 256, 0.0), (NS + 256, 256, 0.0)]
                    for (s0, sn, init) in segs:
                        if d == 0:
                            P.op("dve", lambda e, Hd=Hd, s0=s0, sn=sn, init=init, R=R, I=I: e.tensor_tensor_scan(
                                out=Hd[:, s0:s0 + sn], data0=R[:, s0:s0 + sn], data1=I[:, s0:s0 + sn], initial=init, op0=ALU.mult, op1=ALU.add),
                                reads=[R_t, I_t, pv_t], writes=[H_t[d]])
                        else:
                            P.op("dve", lambda e, Hd=Hd, s0=s0, sn=sn, init=init, R=R, I=I: e.tensor_tensor_scan(
                                out=rev(Hd[:, s0:s0 + sn]), data0=rev(R[:, s0:s0 + sn]), data1=rev(I[:, s0:s0 + sn]), initial=init,
                                op0=ALU.mult, op1=ALU.add), reads=[R_t, I_t, pv_t], writes=[H_t[d]])
                    for s_ in range(2):
                        col = ((s_ * NL + l) * 2 + d) * 4 + c
                        srcc = NS + s_ * 256 + (255 if d == 0 else 0)
                        P.op("pool", lambda e, Hd=Hd, col=col, srcc=srcc: e.tensor_copy(out=fin[:, col:col + 1], in_=Hd[:, srcc:srcc + 1]),
                             reads=[H_t[d]], writes=[fin_t])
                P.op("dve", lambda e: e.tensor_tensor(out=H[0], in0=H[0], in1=H[1], op=ALU.add), reads=[H_t[0], H_t[1]], writes=[H_t[0]])
                P.op("dve", lambda e, UGb=UGb: e.tensor_tensor(out=ob, in0=H[0], in1=UGb, op=ALU.mult), reads=[H_t[0], UG_t], writes=[ob_t])
                P.dma("sp", LO[c * 128:(c + 1) * 128, :], ob, reads=[ob_t])

        def write_fin():
            P.op("pe", lambda e: e.transpose(ps[7][0:32, 0:128], fin, ident), reads=[fin_t, ident_t], writes=[pst[7]])
            fo = PA.alloc(128)
            fo_t = T("fo")
            P.op("act", lambda e: e.copy(out=fo[0:32, :], in_=ps[7][0:32, 0:128]), reads=[pst[7]], writes=[fo_t])
            P.dma("sp", nh_d, fo[0:32, :], reads=[fo_t])

        def phase_T(l):
            A.top = 0
            Vs = A.alloc(32 * 520 + 64, BF16)
            Vc = A.alloc(4 * 520 + 64, BF16)
            Vp = A.alloc(4 * 520 + 64, BF16)
            V_t = [T("Vt%d" % i) for i in range(14)]
            P.op("pool", lambda e: e.memset(Vc, 1.0), writes=[V_t[5:13]])
            P.op("pool", lambda e: e.memset(Vs[:, 32 * 520:32 * 520 + 64], 1.0), writes=[V_t[13]])
            P.op("pool", lambda e: e.memset(Vp[:, 4 * 520:4 * 520 + 64], 1.0), writes=[V_t[13]])
            for g4 in range(4):
                P.dma("sp", V(Vs, g4 * 8 * 520, [(520, 8), (1, 520)]),
                      VS[g4 * 1024:(g4 + 1) * 1024, :].rearrange("(pr p) n -> p pr n", p=128), writes=[V_t[g4]])
            P.dma("sp", V(Vp, 0, [(520, 4), (1, 520)]), VS[NS:NT, :].rearrange("(pr p) n -> p pr n", p=128), writes=[V_t[4]])
            for h in range(8):
                P.dma("pool", V(Vc, h * 65, [(520, 4), (1, 64)]),
                      cv_d[l][:, h * 64:(h + 1) * 64].rearrange("(kc p) d -> p kc d", p=128), writes=[V_t[5 + h]])
            ckt = A.alloc(4 * 512)
            ckt_t = T("ckt")
            P.dma("sp", V(ckt, 0, [(512, 4), (1, 512)]), ck_d[l].rearrange("(kc p) n -> p kc n", p=128), writes=[ckt_t])
            KcTz = [[A.alloc(512, BF16) for _ in range(4)] for _ in range(2)]
            KcT_t = T("KcT")
            for hp in range(2):
                for c in range(4):
                    P.op("pool", lambda e, hp=hp, c=c: e.memset(KcTz[hp][c], 0.0), writes=[KcT_t])
            for c in range(4):
                for kc in range(4):
                    P.op("pe", lambda e, c=c, kc=kc: e.transpose(ps[c][:, kc * 128:(kc + 1) * 128],
                                                               ckt[:, kc * 512 + c * 128: kc * 512 + c * 128 + 128], ident),
                         reads=[ckt_t, ident_t], writes=[pst[c]], inc=(kc == 3))
                P.op("act", lambda e, c=c: e.copy(out=KcTz[0][c][0:64, :], in_=ps[c][0:64, :]), writes=[pst[c], KcT_t])
                P.op("act", lambda e, c=c: e.copy(out=KcTz[1][c][64:128, :], in_=ps[c][64:128, :]), writes=[pst[c], KcT_t])
            Mint = A.alloc(NY * 64, BF16)
            Medge = A.alloc(12 * 512, BF16)
            M_t = [T("masks0"), T("masks1"), T("masks2")]
            P.dma("pool", Mint, mint_d, writes=[M_t[0]])
            for mh in range(2):
                P.dma("pool", V(Medge, mh * 6 * 512, [(512, 6), (1, 512)]), medge_d[mh * 6:(mh + 1) * 6].rearrange("j p n -> p j n"), writes=[M_t[1 + mh]])
            import os as _os2
            KTS = int(_os2.environ.get("KTS", "9"))
            if KTS <= 1:
                return
            Gh = [A.alloc(NY * 64) for _ in range(2)]
            Tint = [A.alloc(NY * 64, BF16) for _ in range(2)]
            Tedge = [A.alloc(12 * 512, BF16) for _ in range(2)]
            Gh_t = [[T("Gh%d_%d" % (a_, b_)) for b_ in range(4)] for a_ in range(2)]
            Tb_t = [T("Tb0"), T("Tb1")]
            QTc = [A.alloc(NT, BF16) for _ in range(2)]
            KTz = [[A.alloc(NT, BF16) for _ in range(2)] for _ in range(2)]
            QK_t = [T("QK0"), T("QK1")]
            for cb_ in range(2):
                P.op("pool", lambda e, cb_=cb_: e.memset(KTz[cb_][0][64:128, :], 0.0), writes=[QK_t[cb_]])
                P.op("pool", lambda e, cb_=cb_: e.memset(KTz[cb_][1][0:64, :], 0.0), writes=[QK_t[cb_]])
            Sb = [A.alloc(512) for _ in range(2)]
            Sb_t = [T("Sb0"), T("Sb1")]
            Pt = [A.alloc(512, BF16) for _ in range(5)]
            Pt_t = [T("Pt%d" % i) for i in range(5)]
            denrow = A.alloc(512)
            rden = A.alloc(512)
            den_t, rden_t = T("den"), T("rden")
            stage = [A.alloc(NT, BF16) for _ in range(2)]
            stage_t = [T("ast0"), T("ast1")]
            cnt = {"s": 0, "p": 0, "q": 0, "b": 0}

            def finalize(po, n, out_ap, out_t):
                P.op("act", lambda e: e.copy(out=denrow[64:65, 0:n], in_=ps[po][64:65, 0:n]), writes=[pst[po], den_t])
                P.op("pe", lambda e: e.matmul(ps[6][0:64, 0:n], lhsT=ones_f[64:65, 0:64], rhs=denrow[64:65, 0:n], start=True, stop=True),
                     reads=[den_t, cst_t], writes=[pst[6]])
                P.op("dve", lambda e: e.reciprocal(out=rden[0:64, 0:n], in_=ps[6][0:64, 0:n]), reads=[pst[6]], writes=[rden_t])
                P.op("dve", lambda e: e.tensor_tensor(out=out_ap, in0=ps[po][0:64, 0:n], in1=rden[0:64, 0:n], op=ALU.mult),
                     reads=[rden_t], writes=[pst[po], out_t])

            LOOK = 4
            pres = {}
            early_idx = {}
            head_first = {}
            SBANKS = [0, 1, 2, 3, 7]
            NSB = 5
            NPT = 5
            tiles = []
            for h in range(8):
                c = h // 2
                hb = (h % 2) * 64
                tb = h % 2
                cb = c % 2

                def pre(h=h, c=c, tb=tb, cb=cb):
                    if h % 2 == 0:
                        P.dma("sp", QTc[cb], QT[c * 128:(c + 1) * 128, :], writes=[QK_t[cb]])
                        P.dma("sp", KTz[cb][0][0:64, :], KT[c * 128:c * 128 + 64, :], writes=[QK_t[cb]])
                        P.dma("sp", KTz[cb][1][64:128, :], KT[c * 128 + 64:(c + 1) * 128, :], writes=[QK_t[cb]])
                    for half in range(2):
                        for yh in range(2):
                            src = bass.AP(rbp_d.tensor, rbp_d.offset + l * RB_LEN + RB_PAD + h * 465 + (-3 - half + yh * 11) * 31 - 48,
                                          [[1, 64], [31, 11], [1, 64]])
                            P.dma("sp", V(Gh[tb], yh * 11 * 64, [(64, 11), (1, 64)], p0=half * 64, pn=64), src, writes=[Gh_t[tb][half * 2 + yh]])
                    P.op("dve", lambda e: e.scalar_tensor_tensor(out=V(Tint[tb], 0, [(64, NY), (1, 64)]), in0=V(Gh[tb], 63, [(64, NY), (-1, 64)]),
                                                                 scalar=8.0, in1=V(Mint, 0, [(64, NY), (1, 64)]), op0=ALU.mult, op1=ALU.add),
                         reads=[Gh_t[tb], M_t], writes=[Tb_t[tb]])
                    for blk, (rq0, krs) in enumerate(((0, TOP_KR), (56, BOT_KR))):
                        for j, kr in enumerate(krs):
                            y0 = E_MAX - (kr - rq0)
                            jj = blk * 6 + j
                            P.op("dve", lambda e, y0=y0, jj=jj: e.scalar_tensor_tensor(
                                out=V(Tedge[tb], jj * 512, [(64, 8), (1, 64)]), in0=V(Gh[tb], y0 * 64 + 63, [(64, 8), (-1, 64)]),
                                scalar=8.0, in1=V(Medge, jj * 512, [(64, 8), (1, 64)]), op0=ALU.mult, op1=ALU.add),
                                reads=[Gh_t[tb], M_t], writes=[Tb_t[tb]])

                pres[h] = pre
                head_first[h] = len(tiles)
                first_of_head = (h == 0)
                for qb in range(8):
                    rq0 = 8 * qb
                    early_mark = len(tiles) if qb == 3 else None
                    if early_mark is not None:
                        early_idx[h] = early_mark
                    if qb == 0:
                        chunks = [(kr, Tedge[tb][:, j * 512:(j + 1) * 512]) for j, kr in enumerate(TOP_KR)]
                    elif qb == 7:
                        chunks = [(kr, Tedge[tb][:, (6 + j) * 512:(7 + j) * 512]) for j, kr in enumerate(BOT_KR)]
                    else:
                        chunks = []
                        for kr in range(rq0 - 4, rq0 + 12, 2):
                            y0 = E_MAX - (kr - rq0)
                            chunks.append((kr, Tint[tb][:, y0 * 64:(y0 + 8) * 64]))
                    q_ap = QTc[cb][:, rq0 * 64: rq0 * 64 + 512]
                    po = 4 + cnt["q"] % 2
                    cnt["q"] += 1
                    ntot = len(chunks) + 4
                    k = 0
                    for kc in range(4):
                        tiles.append(dict(pre=pre if first_of_head else None, n=512, c0=0, q=q_ap, tb=tb, cb=cb,
                                          kT=KcTz[h % 2][c][:, kc * 128:(kc + 1) * 128], kT_t=[QK_t[cb], KcT_t], tbl=None,
                                          vl=V(Vc, kc * 520 + h * 65, [(1, 128)]), po=po, first=(k == 0), last=(k == ntot - 1),
                                          out=stage[tb][0:64, rq0 * 64: rq0 * 64 + 512], post=None))
                        first_of_head = False
                        k += 1
                    for (kr, tbl) in chunks:
                        rows = [r for r in range(8) if any(_r0(rq0 + r) <= kr + hf <= _r0(rq0 + r) + 7 for hf in range(2))]
                        ra, rb = min(rows), max(rows)
                        assert rows == list(range(ra, rb + 1))
                        c0, n = ra * 64, (rb - ra + 1) * 64
                        tiles.append(dict(pre=None, n=n, c0=c0, q=q_ap[:, c0:c0 + n], tb=tb, cb=cb,
                                          kT=KTz[cb][h % 2][:, kr * 64: kr * 64 + 128], kT_t=[QK_t[cb]], tbl=tbl[:, c0:c0 + n],
                                          vl=V(Vs, (kr // 2) * 520 + h * 65, [(1, 128)]), po=po, first=(k == 0), last=(k == ntot - 1),
                                          out=stage[tb][0:64, rq0 * 64: rq0 * 64 + 512], post=None))
                        k += 1
                for s_ in range(2):
                    q_ap = QTc[cb][:, NS + s_ * 256: NS + s_ * 256 + 256]
                    po = 4 + cnt["q"] % 2
                    cnt["q"] += 1
                    for kc in range(2):
                        k0 = NS + s_ * 256 + kc * 128
                        post = None
                        if s_ == 1 and kc == 1:
                            def post(h=h, tb=tb):
                                P.dma("sp", AO[h * 64:(h + 1) * 64, :], stage[tb][0:64, :], reads=[stage_t[tb]])
                        tiles.append(dict(pre=None, n=256, c0=0, q=q_ap, tb=tb, cb=cb,
                                          kT=KTz[cb][h % 2][:, k0:k0 + 128], kT_t=[QK_t[cb]], tbl=None,
                                          vl=V(Vp, (s_ * 2 + kc) * 520 + h * 65, [(1, 128)]), po=po, first=(kc == 0), last=(kc == 1),
                                          out=stage[tb][0:64, NS + s_ * 256: NS + s_ * 256 + 256], post=post))

            deferred = []

            def fin_a(t):
                po, n = t["po"], t["gn"]
                P.op("dve", lambda e: e.tensor_copy(out=denrow[64:65, 0:n], in_=ps[po][64:65, 0:n]), writes=[pst[po], den_t])

            def fin_b(t):
                po, n, out_ap, tb = t["po"], t["gn"], t["out"], t["tb"]
                P.op("pe", lambda e: e.matmul(ps[6][0:64, 0:n], lhsT=ones_f[64:65, 0:64], rhs=denrow[64:65, 0:n], start=True, stop=True),
                     reads=[den_t, cst_t], writes=[pst[6]])
                P.op("dve", lambda e: e.reciprocal(out=rden[0:64, 0:n], in_=ps[6][0:64, 0:n]), reads=[pst[6]], writes=[rden_t])
                P.op("dve", lambda e: e.tensor_tensor(out=out_ap, in0=ps[po][0:64, 0:n], in1=rden[0:64, 0:n], op=ALU.mult),
                     reads=[rden_t], writes=[pst[po], stage_t[tb]])
                if t["post"] is not None:
                    t["post"]()

            for h_ in range(7):
                k_ = head_first[h_ + 1] - 8
                assert tiles[k_]["pre"] is None
                tiles[k_]["pre"] = pres[h_ + 1]
            NTL = len(tiles)
            for t in tiles:
                t["gn"] = 256 if t["vl"].tensor is Vp.tensor and False else None
            gw = None
            for t in tiles:
                if t["first"]:
                    gw = t["n"]
                t["gn"] = gw
            for idx in range(NTL + LOOK + 3):
                while deferred and deferred[0][0] <= idx:
                    fin_b(deferred.pop(0)[1])
                if idx < NTL:
                    t = tiles[idx]
                    if t["pre"] is not None:
                        t["pre"]()
                    sb_ = SBANKS[idx % NSB]
                    n = t["n"]
                    loc = t["tbl"] is not None
                    P.op("pe", lambda e, t=t, sb_=sb_, n=n, loc=loc: e.matmul(ps[sb_][:, 0:n], lhsT=t["kT"], rhs=t["q"], start=True, stop=not loc),
                         reads=t["kT_t"], writes=[pst[sb_]], inc=not loc)
                    if loc:
                        P.op("pe", lambda e, t=t, sb_=sb_, n=n: e.matmul(ps[sb_][:, 0:n], lhsT=ident_b, rhs=t["tbl"], start=False, stop=True),
                             reads=[cst_t, Tb_t[t["tb"]]], writes=[pst[sb_]])
                j = idx - LOOK
                if 0 <= j < NTL:
                    t = tiles[j]
                    sb_ = SBANKS[j % NSB]
                    p_ = j % NPT
                    n = t["n"]
                    P.op("act", lambda e, sb_=sb_, p_=p_, n=n: e.activation(out=Pt[p_][:, 0:n], in_=ps[sb_][:, 0:n], func=AF.Exp, scale=0.125),
                         reads=[pst[sb_]], writes=[Pt_t[p_]])
                    P.op("pe", lambda e, t=t, p_=p_, n=n: e.matmul(ps[t["po"]][:, t["c0"]:t["c0"] + n], lhsT=t["vl"], rhs=Pt[p_][:, 0:n],
                                                                  start=t["first"], stop=t["last"]),
                         reads=[V_t, Pt_t[p_]], writes=[pst[t["po"]]])
                    if t["last"]:
                        fin_a(t)
                        deferred.append((idx + 2, t))
            assert not deferred

        def phase_C1(l):
            A.top = 0
            wbr = A.alloc(3 * 4096, BF16)
            wo = A.alloc(8 * 1024, BF16)
            w_t = [T("c1w%d" % i) for i in range(5)]
            for b in range(3):
                P.dma("pool", V(wbr, b * 4096, [(1024, 4), (1, 1024)]), wbr_d[l, b].rearrange("(kc p) n -> p kc n", p=128), writes=[w_t[b]])
            for hf in range(2):
                P.dma("pool", V(wo, hf * 4096, [(1024, 4), (1, 1024)]),
                      wout_d[l, hf * 512:(hf + 1) * 512, :].rearrange("(kc p) n -> p kc n", p=128), writes=[w_t[3 + hf]])
            Bt = [[A.alloc(4 * TT, BF16) for _ in range(3)] for _ in range(2)]
            Bt_t = [[T("Bt%d%d" % (a_, b)) for b in range(3)] for a_ in range(2)]
            xT3 = [A.alloc(8 * TT) for _ in range(3)]
            xT3_t = [T("c1x0"), T("c1x1"), T("c1x2")]
            Gt = [A.alloc(3 * TT, BF16) for _ in range(3)]
            Gt_t = [T("Gt0"), T("Gt1"), T("Gt2")]
            mg = [A.alloc(8 * TT, BF16) for _ in range(2)]
            mg_t = [T("mg0"), T("mg1")]
            tmpb = [[A.alloc(TT) for _ in range(3)] for _ in range(2)]
            tmpb_t = [[T("tmp%d%d" % (a_, b)) for b in range(3)] for a_ in range(2)]
            accb = [A.alloc(TT) for _ in range(2)]
            accb_t = [T("acc0"), T("acc1")]
            sq = A.alloc(8 * TT, BF16)
            sq_t = T("c1sq")
            rs = A.alloc(TT)
            rs_t = T("c1rs")
            xn = A.alloc(8 * TT)
            xn_t = T("c1xn")
            h2s = [A.alloc(8 * TT, BF16) for _ in range(2)]
            h2s_t = [T("h2s0"), T("h2s1")]
            st = {"gn": 0, "pbn": 0}
            GSr = GS.rearrange("(b r) t -> r b t", b=3)

            def loads_B(i):
                b_ = i % 2
                for bi, src in enumerate((PO, AO, LO)):
                    P.dma("sp", V(Bt[b_][bi], 0, [(TT, 4), (1, TT)]), src[:, i * TT:(i + 1) * TT].rearrange("(kc p) t -> p kc t", p=128),
                          writes=[Bt_t[b_][bi]])

            def load_x(i):
                xi = i % 3
                P.dma("sp", V(xT3[xi], 0, [(TT, 8), (1, TT)]), XT[:, i * TT:(i + 1) * TT].rearrange("(c p) t -> p c t", p=128),
                      writes=[xT3_t[xi]])

            def branch(i):
                b_ = i % 2
                for m in range(8):
                    g_ = st["gn"] % 3
                    st["gn"] += 1
                    P.dma("sp", V(Gt[g_], 0, [(TT, 3), (1, TT)]), GSr[m * 128:(m + 1) * 128, :, i * TT:(i + 1) * TT], writes=[Gt_t[g_]])
                    tmp, tmp_t, acc, acc_t = tmpb[m % 2], tmpb_t[m % 2], accb[m % 2], accb_t[m % 2]
                    for bi in range(3):
                        pb = st["pbn"] % 5
                        st["pbn"] += 1
                        for kc in range(4):
                            P.op("pe", lambda e, pb=pb, bi=bi, kc=kc, m=m, b_=b_: e.matmul(
                                ps[pb], lhsT=wbr[:, bi * 4096 + kc * 1024 + m * 128: bi * 4096 + kc * 1024 + m * 128 + 128],
                                rhs=Bt[b_][bi][:, kc * TT:(kc + 1) * TT], start=(kc == 0), stop=(kc == 3)),
                                reads=[w_t[bi], Bt_t[b_][bi]], writes=[pst[pb]], inc=(kc == 3))
                        P.op("dve", lambda e, pb=pb, bi=bi, g_=g_, tmp=tmp: e.tensor_tensor(out=tmp[bi], in0=ps[pb], in1=Gt[g_][:, bi * TT:(bi + 1) * TT],
                                                                                            op=ALU.mult),
                             reads=[pst[pb], Gt_t[g_]], writes=[tmp_t[bi]])
                    P.op("pool", lambda e, tmp=tmp, acc=acc: e.tensor_tensor(out=acc, in0=tmp[0], in1=tmp[1], op=ALU.add),
                         reads=[tmp_t[0], tmp_t[1]], writes=[acc_t])
                    P.op("pool", lambda e, m=m, b_=b_, tmp=tmp, acc=acc: e.tensor_tensor(out=mg[b_][:, m * TT:(m + 1) * TT], in0=acc, in1=tmp[2], op=ALU.add),
                         reads=[acc_t, tmp_t[2]], writes=[mg_t[b_]])

            def outproj(i):
                b_ = i % 2
                xi = i % 3
                j = 0 if i < 8 else 1
                for m2 in range(8):
                    pb = 5 + m2 % 2
                    for kc in range(8):
                        P.op("pe", lambda e, pb=pb, kc=kc, m2=m2, b_=b_: e.matmul(
                            ps[pb], lhsT=wo[:, kc * 1024 + m2 * 128: kc * 1024 + m2 * 128 + 128], rhs=mg[b_][:, kc * TT:(kc + 1) * TT],
                            start=(kc == 0), stop=(kc == 7)), reads=[w_t[3 + kc // 4], mg_t[b_]], writes=[pst[pb]], inc=(kc == 7))
                    P.op("dve", lambda e, xi=xi, pb=pb, m2=m2, j=j: e.scalar_tensor_tensor(
                        out=xT3[xi][:, m2 * TT:(m2 + 1) * TT], in0=ps[pb], scalar=modcol(l, j, 2, m2), in1=xT3[xi][:, m2 * TT:(m2 + 1) * TT],
                        op0=ALU.mult, op1=ALU.add), reads=[pst[pb], xT3_t[xi], mod_t[l]], writes=[xT3_t[xi]])
                P.dma("act", XT[:, i * TT:(i + 1) * TT].rearrange("(c p) t -> p c t", p=128), V(xT3[xi], 0, [(TT, 8), (1, TT)]), reads=[xT3_t[xi]])

            def norm2_part(i):
                b_ = i % 2
                xi = i % 3
                j = 0 if i < 8 else 1
                pn = 7
                P.op("act", lambda e, xi=xi: e.activation(out=sq, in_=xT3[xi], func=AF.Square), reads=[xT3_t[xi]], writes=[sq_t])
                for c in range(8):
                    P.op("pe", lambda e, c=c, pn=pn: e.matmul(ps[pn], lhsT=ones_b, rhs=sq[:, c * TT:(c + 1) * TT], start=(c == 0), stop=(c == 7)),
                         reads=[sq_t, cst_t], writes=[pst[pn]], inc=(c == 7))
                P.op("act", lambda e, pn=pn: e.activation(out=rs, in_=ps[pn], func=AF.Ln, scale=1.0 / D, bias=epsc),
                     reads=[pst[pn], cst_t], writes=[rs_t])
                P.op("act", lambda e: e.activation(out=rs, in_=rs, func=AF.Exp, scale=-0.5), reads=[rs_t], writes=[rs_t])
                P.op("dve", lambda e, xi=xi: e.tensor_tensor(out=V(xn, 0, [(TT, 8), (1, TT)]), in0=V(xT3[xi], 0, [(TT, 8), (1, TT)]),
                                                             in1=V(rs, 0, [(0, 8), (1, TT)]), op=ALU.mult),
                     reads=[rs_t, xT3_t[xi]], writes=[xn_t])
                for c in range(8):
                    P.op("act", lambda e, c=c, b_=b_, j=j: e.activation(
                        out=h2s[b_][:, c * TT:(c + 1) * TT], in_=xn[:, c * TT:(c + 1) * TT], func=AF.Identity,
                        scale=A2[l][:, j * 8 + c: j * 8 + c + 1], bias=modcol(l, j, 3, c)),
                        reads=[xn_t, mod_t[l]], writes=[h2s_t[b_]])
                P.dma("act", H2[:, i * TT:(i + 1) * TT].rearrange("(c p) t -> p c t", p=128), V(h2s[b_], 0, [(TT, 8), (1, TT)]), reads=[h2s_t[b_]])

            loads_B(0)
            load_x(0)
            for i in range(NTILE + 2):
                if i + 1 < NTILE:
                    loads_B(i + 1)
                if i < NTILE:
                    branch(i)
                if 0 <= i - 2 < NTILE:
                    norm2_part(i - 2)
                if i + 1 < NTILE:
                    load_x(i + 1)
                if 0 <= i - 1 < NTILE:
                    outproj(i - 1)

        def phase_C2(l, last):
            A.top = 0
            wup = A.alloc(8 * 4096, BF16)
            wdn = A.alloc(32 * 1024, BF16)
            wu_t = [T("c2wu%d" % i) for i in range(4)]
            wd_t = [T("c2wd%d" % i) for i in range(4)]
            w_t = wu_t
            for q in range(4):
                P.dma("pool", V(wup, q * 1024, [(4096, 8), (1, 1024)]),
                      wup_d[l, :, q * 1024:(q + 1) * 1024].rearrange("(kc p) n -> p kc n", p=128), writes=[wu_t[q]])
            for q in range(4):
                P.dma("pool", V(wdn, q * 8192, [(1024, 8), (1, 1024)]),
                      wdn_d[l, q * 1024:(q + 1) * 1024, :].rearrange("(fc p) n -> p fc n", p=128), writes=[wd_t[q]])
            h2tb = [A.alloc(8 * TT, BF16) for _ in range(2)]
            x1t = A.alloc(8 * TT)
            a = A.alloc(32 * TT, BF16)
            h2tt = [T("h2t0"), T("h2t1")]
            x1t_t = T("x1t")
            a_t = [T("a%d" % f) for f in range(32)]
            yo = A.alloc(1024) if last else None
            yo_t = T("yo")
            P.dma("sp", V(h2tb[0], 0, [(TT, 8), (1, TT)]), H2[:, 0:TT].rearrange("(c p) t -> p c t", p=128), writes=[h2tt[0]])
            for i in range(NTILE):
                j = 0 if i < 8 else 1
                h2t, h2t_t = h2tb[i % 2], h2tt[i % 2]
                P.dma("sp", V(x1t, 0, [(TT, 8), (1, TT)]), XT[:, i * TT:(i + 1) * TT].rearrange("(c p) t -> p c t", p=128), writes=[x1t_t])
                if i + 1 < NTILE:
                    P.dma("sp", V(h2tb[(i + 1) % 2], 0, [(TT, 8), (1, TT)]), H2[:, (i + 1) * TT:(i + 2) * TT].rearrange("(c p) t -> p c t", p=128),
                          writes=[h2tt[(i + 1) % 2]])
                for f in range(32):
                    pb = f % 4
                    for kc in range(8):
                        P.op("pe", lambda e, pb=pb, f=f, kc=kc, h2t=h2t: e.matmul(
                            ps[pb], lhsT=wup[:, kc * 4096 + f * 128: kc * 4096 + f * 128 + 128], rhs=h2t[:, kc * TT:(kc + 1) * TT],
                            start=(kc == 0), stop=(kc == 7)), reads=[wu_t[f // 8], h2t_t], writes=[pst[pb]], inc=(kc == 7))
                    P.op("act", lambda e, pb=pb, f=f: e.activation(out=a[:, f * TT:(f + 1) * TT], in_=ps[pb], func=AF.Relu),
                         reads=[pst[pb]], writes=[a_t[f]])
                    eng = "dve" if f % 2 == 0 else "pool"
                    P.op(eng, lambda e, f=f: e.tensor_tensor(out=a[:, f * TT:(f + 1) * TT], in0=a[:, f * TT:(f + 1) * TT],
                                                             in1=a[:, f * TT:(f + 1) * TT], op=ALU.mult),
                         reads=[a_t[f]], writes=[a_t[f]])
                for m in range(8):
                    pb = 4 + m % 2
                    for fc in range(32):
                        P.op("pe", lambda e, pb=pb, fc=fc, m=m: e.matmul(
                            ps[pb], lhsT=wdn[:, fc * 1024 + m * 128: fc * 1024 + m * 128 + 128], rhs=a[:, fc * TT:(fc + 1) * TT],
                            start=(fc == 0), stop=(fc == 31)), reads=[wd_t[fc // 8], a_t[fc]], writes=[pst[pb]], inc=(fc == 31))
                    P.op("dve", lambda e, pb=pb, m=m, j=j: e.scalar_tensor_tensor(
                        out=x1t[:, m * TT:(m + 1) * TT], in0=ps[pb], scalar=modcol(l, j, 5, m), in1=x1t[:, m * TT:(m + 1) * TT],
                        op0=ALU.mult, op1=ALU.add), reads=[pst[pb], x1t_t, mod_t[l]], writes=[x1t_t])
                if not last:
                    P.dma("sp", XT[:, i * TT:(i + 1) * TT].rearrange("(c p) t -> p c t", p=128), V(x1t, 0, [(TT, 8), (1, TT)]), reads=[x1t_t])
                else:
                    for st in range(4):
                        for m in range(8):
                            pb = 6 + m // 4
                            P.op("pe", lambda e, pb=pb, m=m, st=st: e.transpose(
                                ps[pb][:, (m % 4) * 128:(m % 4) * 128 + 128], x1t[:, m * TT + st * 128: m * TT + st * 128 + 128], ident),
                                reads=[x1t_t, ident_t], writes=[pst[pb]], inc=(m % 4 == 3))
                        P.op("act", lambda e: e.copy(out=yo[:, 0:512], in_=ps[6]), reads=[pst[6]], writes=[yo_t])
                        P.op("dve", lambda e: e.tensor_copy(out=yo[:, 512:1024], in_=ps[7]), reads=[pst[7]], writes=[yo_t])
                        P.dma("sp", y_d[i * TT + st * 128: i * TT + st * 128 + 128, :], yo, reads=[yo_t])

        if stop != "pre":
            phase_M()
        P.barrier()
        if stop not in ("M", "pre", "M1", "M2", "M3"):
            for l in range(nl):
                last = (l == nl - 1)
                phase_A(l)
                P.barrier()
                if last and stop in ("A", "A0"):
                    break
                phase_P(l)
                P.barrier()
                if last and stop == "P":
                    break
                phase_Q(l)
                P.barrier()
                if last and stop == "Q":
                    break
                phase_L(l)
                P.barrier()
                if last and stop == "L":
                    write_fin()
                    break
                phase_T(l)
                P.barrier()
                if last and stop == "T":
                    break
                phase_C1(l)
                P.barrier()
                if last and stop == "C1":
                    break
                phase_C2(l, last and stop != "C2")
                P.barrier()
                if last:
                    write_fin()
        P.barrier()
        P.emit(lambda name: es.enter_context(nc.semaphore(name)))
    return nc


def host_inputs(inp, consts=None):
    if consts is None:
        consts = host_consts()
    f = np.float32
    shared = {k: np.ascontiguousarray(inp[k], dtype=f) for k in ("w_mod", "w_in", "w_branch", "w_out", "w_up", "w_down", "w_pool")}
    lbd = np.zeros((NL, 2, 2, 4, 128, 128), f)
    for ai, name in enumerate(("lru_w_a", "lru_w_i")):
        w = inp[name]
        for c in range(4):
            for hb in range(2):
                lbd[:, :, ai, c, hb * 64:(hb + 1) * 64, hb * 64:(hb + 1) * 64] = w[:, :, 2 * c + hb]
    shared["lru_bd"] = lbd
    rbp = np.zeros((NL, RB_LEN), f)
    rbp[:, RB_PAD:RB_PAD + 8 * 15 * 31] = inp["rel_bias"][:, :, ::-1, :].reshape(NL, -1)
    shared["rbp"] = rbp
    for k in ("ivt", "ident", "bones", "permf", "ropeC", "ropeS", "mint", "medge", "pinv"):
        shared[k] = consts[k]
    perm = np.arange(128)
    perm = (perm // 16 ^ 1) * 16 + perm % 16
    maps = []
    for b in range(8):
        m = dict(shared)
        m["x"] = np.ascontiguousarray(np.concatenate([inp["x_sample"][b], inp["x_prompt"][2 * b], inp["x_prompt"][2 * b + 1]], axis=0), dtype=f)
        m["ck"] = np.ascontiguousarray(inp["cache_k"][b].reshape(NL, 512, 512), dtype=f)
        m["cv"] = np.ascontiguousarray(inp["cache_v"][b].reshape(NL, 512, 512), dtype=f)
        pvec = np.zeros((128, NPV), f)

        def put(name, l, arr):
            o, w = PV[(name, l)]
            pvec[:, o:o + w] = arr

        for l in range(NL):
            put("bmod", l, inp["b_mod"][l].reshape(48, 128).T)
            put("g1", l, inp["norm1_g"][l].reshape(8, 128).T)
            put("g2", l, inp["norm2_g"][l].reshape(8, 128).T)
            qg = np.tile(inp["q_norm_g"][l], 2)
            kg = np.tile(inp["k_norm_g"][l], 2)
            put("qg", l, qg[:, None])
            put("qgp", l, qg[perm][:, None])
            put("kg", l, kg[:, None])
            put("kgp", l, kg[perm][:, None])
            put("pscale", l, inp["pool_scale"][l].reshape(4, 128).T)
            put("convw", l, inp["conv_w"][l].reshape(4, 4, 128).transpose(2, 0, 1).reshape(128, 16))
            put("convb", l, inp["conv_b"][l].reshape(4, 128).T)
            put("ba", l, inp["lru_b_a"][l].reshape(2, 4, 128).transpose(2, 0, 1).reshape(128, 8))
            put("bi", l, inp["lru_b_i"][l].reshape(2, 4, 128).transpose(2, 0, 1).reshape(128, 8))
            put("lam", l, inp["lru_lambda"][l].reshape(2, 4, 128).transpose(2, 0, 1).reshape(128, 8))
            put("h0", l, inp["state_h"][b, l].reshape(2, 4, 128).transpose(2, 0, 1).reshape(128, 8))
        o, _ = PV["cond"]
        cs = inp["c"][b].reshape(8, 128).T
        cc = inp["c_ctx"].reshape(8, 128).T
        pvec[:, o:o + 16:2] = cs
        pvec[:, o + 1:o + 16:2] = cc
        m["pvec"] = pvec
        maps.append(m)
    return maps


_NC_CACHE = {}


def kernel(**inputs):
    inp = {k: np.asarray(v) for k, v in inputs.items()}
    if "nc" not in _NC_CACHE:
        _NC_CACHE["nc"] = build()
    nc = _NC_CACHE["nc"]
    maps = host_inputs(inp)
    res = run_bass_kernel_spmd(nc, maps, core_ids=list(range(8)))
    y_prompt = np.zeros((16, 256, D), np.float32)
    y_sample = np.zeros((8, NS, D), np.float32)
    nk = np.zeros((16, NL, 256, 8, 64), np.float32)
    nv = np.zeros((16, NL, 256, 8, 64), np.float32)
    nh = np.zeros((16, NL, 2, 512), np.float32)
    for b in range(8):
        r = res.results[b]
        y = r["y"]
        y_sample[b] = y[:NS]
        y_prompt[2 * b] = y[NS:NS + 256]
        y_prompt[2 * b + 1] = y[NS + 256:]
        nk[2 * b:2 * b + 2] = r["nk"].reshape(2, NL, 256, 8, 64)
        nv[2 * b:2 * b + 2] = r["nv"].reshape(2, NL, 256, 8, 64)
        nh[2 * b:2 * b + 2] = r["nh"].reshape(2, NL, 2, 512)
    return (y_prompt, y_sample, nk, nv, nh)
```
